# Optimizing a Trainium2 kernel written in Bass

```python
import jax, jax.numpy as jnp
from jax import lax
import numpy as np

D_MODEL = 2048
BATCH = 4
SEQ = 8192
DEPTH = 1

POOL_WINDOWS = (2, 4, 8, 16)
POOL_GROUPS = 4
POOL_WIDTH = D_MODEL // 2
POOL_GROUP_DIM = POOL_WIDTH // POOL_GROUPS
GLA_HEADS = 4
GLA_KEY_DIM = D_MODEL // 2
GLA_VALUE_DIM = D_MODEL
GLA_HEAD_K = GLA_KEY_DIM // GLA_HEADS
GLA_HEAD_V = GLA_VALUE_DIM // GLA_HEADS
GLA_GATE_RANK = 16
GLA_GATE_TAU = 16.0
GLA_CHUNK = 64
D_FF = -(-8 * D_MODEL // (3 * 256)) * 256
N_BRANCHES = 2
EPS = 1e-6
IN_SIZES = (POOL_WIDTH, GLA_KEY_DIM, GLA_KEY_DIM, GLA_VALUE_DIM, GLA_GATE_RANK, GLA_VALUE_DIM, N_BRANCHES * D_MODEL)
D_IN = sum(IN_SIZES)

kernel_name = "hybrid_pool_gla_gated_block"


def rms_norm(x, g):
    xf = x.astype(jnp.float32)
    y = xf * lax.rsqrt(jnp.mean(xf * xf, axis=-1, keepdims=True) + EPS)
    return (y * g.astype(jnp.float32)).astype(x.dtype)


def split_combined(z):
    idx = [int(i) for i in np.cumsum(IN_SIZES)[:-1]]
    return jnp.split(z, idx, axis=-1)


def pool_mixer(p, w_pool, pool_scale):
    B, T, _ = p.shape
    pg = p.reshape(B, T, POOL_GROUPS, POOL_GROUP_DIM).astype(jnp.float32)
    cs = jnp.concatenate([jnp.zeros((B, 1, POOL_GROUPS, POOL_GROUP_DIM), jnp.float32),
                          jnp.cumsum(pg, axis=1)], axis=1)
    pos = jnp.arange(T)
    outs = []
    for g, w in enumerate(POOL_WINDOWS):
        c = cs[:, :, g]
        lo = jnp.maximum(pos + 1 - w, 0)
        win_sum = c[:, 1:] - jnp.take(c, lo, axis=1)
        count = (pos + 1 - lo).astype(jnp.float32)
        outs.append(win_sum / count[None, :, None] - pg[:, :, g])
    d = jnp.stack(outs, axis=2).astype(p.dtype)
    y = jnp.einsum('btgc,gcd->btgd', d, w_pool).reshape(B, T, POOL_WIDTH)
    return y * pool_scale


def gla_chunked(q, k, v, log_a):
    B, T, H, dk = q.shape
    dv = v.shape[-1]
    nC = T // GLA_CHUNK
    C = GLA_CHUNK

    def chunks(t):
        return t.astype(jnp.float32).reshape(B, nC, C, H, t.shape[-1]).transpose(1, 0, 3, 2, 4)

    qc = chunks(q) * (dk ** -0.5)
    kc, vc = chunks(k), chunks(v)
    Gc = jnp.cumsum(chunks(log_a), axis=3)
    mask = jnp.tril(jnp.ones((C, C), dtype=bool))

    def step(S, inp):
        qi, ki, vi, Gi = inp
        o_inter = jnp.einsum('bhik,bhkv->bhiv', qi * jnp.exp(Gi), S)
        diff = Gi[:, :, :, None, :] - Gi[:, :, None, :, :]
        decay = jnp.exp(jnp.where(mask[:, :, None], diff, -jnp.inf))
        A = jnp.einsum('bhik,bhjk,bhijk->bhij', qi, ki, decay)
        o_intra = jnp.einsum('bhij,bhjv->bhiv', A, vi)
        G_last = Gi[:, :, -1]
        k_dec = ki * jnp.exp(G_last[:, :, None] - Gi)
        S_new = jnp.exp(G_last)[..., None] * S + jnp.einsum('bhjk,bhjv->bhkv', k_dec, vi)
        return S_new, o_inter + o_intra

    S0 = jnp.zeros((B, H, dk, dv), jnp.float32)
    _, o = lax.scan(step, S0, (qc, kc, vc, Gc))
    return o.transpose(1, 0, 3, 2, 4).reshape(B, T, H, dv).astype(v.dtype)


def setup_inputs(seed: int = 0) -> dict:
    key = jax.random.key(seed)
    ks = jax.random.split(key, 20)
    f32 = jnp.float32
    L = DEPTH

    def nrm(k, shape, fan_in):
        return jax.random.normal(k, shape, f32) * (fan_in ** -0.5)

    def gain(k, shape):
        return 1.0 + 0.05 * jax.random.normal(k, shape, f32)

    return {
        "x": jax.random.normal(ks[0], (BATCH, SEQ, D_MODEL), f32),
        "norm_mix_pre": gain(ks[1], (L, D_MODEL)),
        "w_in": nrm(ks[2], (L, D_MODEL, D_IN), D_MODEL),
        "w_gate_up": nrm(ks[3], (L, GLA_GATE_RANK, GLA_KEY_DIM), GLA_GATE_RANK),
        "b_gate": 0.1 * jax.random.normal(ks[4], (L, GLA_KEY_DIM), f32),
        "w_pool": nrm(ks[5], (L, POOL_GROUPS, POOL_GROUP_DIM, POOL_GROUP_DIM), POOL_GROUP_DIM),
        "pool_scale": gain(ks[6], (L, POOL_WIDTH)),
        "gla_norm": gain(ks[7], (L, GLA_HEAD_V)),
        "w_branch_a": nrm(ks[8], (L, POOL_WIDTH, D_MODEL), POOL_WIDTH),
        "w_branch_b": nrm(ks[9], (L, GLA_VALUE_DIM, D_MODEL), GLA_VALUE_DIM),
        "b_branch_gates": 0.01 * jax.random.normal(ks[10], (L, N_BRANCHES, D_MODEL), f32),
        "w_out": nrm(ks[11], (L, D_MODEL, D_MODEL), D_MODEL),
        "norm_mix_post": gain(ks[12], (L, D_MODEL)),
        "norm_ffn_pre": gain(ks[13], (L, D_MODEL)),
        "w_ffn_gate": nrm(ks[14], (L, D_MODEL, D_FF), D_MODEL),
        "w_ffn_up": nrm(ks[15], (L, D_MODEL, D_FF), D_MODEL),
        "w_ffn_down": nrm(ks[16], (L, D_FF, D_MODEL), D_FF),
        "norm_ffn_post": gain(ks[17], (L, D_MODEL)),
    }


def reference(x, norm_mix_pre, w_in, w_gate_up, b_gate, w_pool, pool_scale, gla_norm,
              w_branch_a, w_branch_b, b_branch_gates, w_out, norm_mix_post,
              norm_ffn_pre, w_ffn_gate, w_ffn_up, w_ffn_down, norm_ffn_post):
    B, T, _ = x.shape
    for l in range(DEPTH):
        h = rms_norm(x, norm_mix_pre[l])
        z = h @ w_in[l]
        p, q, k, v, g_lr, r, gate_logits = split_combined(z)
        y_a = pool_mixer(p, w_pool[l], pool_scale[l]) @ w_branch_a[l]
        log_a = jax.nn.log_sigmoid((g_lr @ w_gate_up[l] + b_gate[l]).astype(jnp.float32)) / GLA_GATE_TAU
        o = gla_chunked(q.reshape(B, T, GLA_HEADS, GLA_HEAD_K),
                        k.reshape(B, T, GLA_HEADS, GLA_HEAD_K),
                        v.reshape(B, T, GLA_HEADS, GLA_HEAD_V),
                        log_a.reshape(B, T, GLA_HEADS, GLA_HEAD_K))
        o = rms_norm(o, gla_norm[l]).reshape(B, T, GLA_VALUE_DIM) * jax.nn.silu(r)
        y_b = o @ w_branch_b[l]
        gates = jax.nn.sigmoid(gate_logits.reshape(B, T, N_BRANCHES, D_MODEL) + b_branch_gates[l])
        mixed = (gates[:, :, 0] * y_a + gates[:, :, 1] * y_b) @ w_out[l]
        x = x + rms_norm(mixed, norm_mix_post[l])
        h = rms_norm(x, norm_ffn_pre[l])
        f = (jax.nn.silu(h @ w_ffn_gate[l]) * (h @ w_ffn_up[l])) @ w_ffn_down[l]
        x = x + rms_norm(f, norm_ffn_post[l])
    return x
```

```python
import math
import os
_STOP = int(os.environ.get('STOPAT', '99'))
_DBG = {}
_STOPB = int(os.environ.get('STOPB', '99'))
from contextlib import ExitStack

import numpy as np
import ml_dtypes
import concourse.bass as bass
import concourse.mybir as mybir
from concourse.bass_utils import run_bass_kernel_spmd

F32 = mybir.dt.float32
BF16 = mybir.dt.bfloat16
AF = mybir.ActivationFunctionType
ALU = mybir.AluOpType

D = 2048
SEQ = 8192
BATCH = 4
NCORE = 8
TOK = 4096
TT = 256
NJ = TT // 128
NT = TOK // TT
DIN = 11280
DFF = 5632
EPS = 1e-6
PAGE = 512
NSLOT = 4
OFF_P, OFF_Q, OFF_K, OFF_V, OFF_G, OFF_R, OFF_GATE = 0, 1024, 2048, 3072, 5120, 5136, 7184
RS_D = 1.0 / math.sqrt(2048.0)
RS_V = 1.0 / math.sqrt(512.0)
LN_QSCALE = math.log(1.0 / 16.0)

C_GPRE, C_PSC, C_GN, C_BG, C_GFFN, NCOLV = 0, 16, 24, 28, 60, 76


class _Eng:
    def __init__(self, name, sem, is_pe=False):
        self.name = name
        self.sem = sem
        self.count = 0
        self.ops = []
        self.seen = {}
        self.is_pe = is_pe


class Sched:
    def __init__(self):
        self.engs = {}
        self.state = {}
        self.dma_counts = {}
        self.dma_issued = {}

    def add_engine(self, name, sem, is_pe=False):
        self.engs[name] = _Eng(name, sem, is_pe=is_pe)

    def _deps(self, eng, reads, writes, cap=None):
        need = {}

        def add(tok, kind):
            if tok is None:
                return
            sem, val, ename = tok
            if ename == eng.name:
                if eng.is_pe:
                    return
            k = id(sem)
            if need.get(k, (None, 0))[1] < val:
                need[k] = (sem, val)

        for key in reads:
            st = self.state.get(key)
            if st is not None:
                add(st[0], "raw")
        for key in writes:
            st = self.state.get(key)
            if st is not None:
                add(st[0], "waw")
                for tok in st[1].values():
                    add(tok, "war")
        waits = []
        for k, (sem, val) in need.items():
            if cap is not None and cap[0] == k and val > cap[1]:
                val = cap[1]
                if val <= 0:
                    continue
            if eng.seen.get(k, 0) >= val:
                continue
            eng.seen[k] = val
            waits.append((sem, val))
        return waits

    def _record(self, tok, reads, writes):
        k = id(tok[0])
        for key in reads:
            st = self.state.get(key)
            if st is None:
                st = self.state[key] = [None, {}]
            old = st[1].get(k)
            if old is None or old[1] < tok[1]:
                st[1][k] = tok
        for key in writes:
            self.state[key] = [tok, {}]

    def op(self, engname, fn, reads=(), writes=(), inc=True):
        eng = self.engs[engname]
        waits = self._deps(eng, reads, writes)
        if engname in ("dve", "pool"):
            for sem_, val_ in waits:
                if sem_ is eng.sem and val_ == eng.count:
                    eng.ops.append(([], lambda e: e.engine_nop(), None, 1))
                    break
        if inc:
            eng.count += 1
            tok = (eng.sem, eng.count, eng.name)
        else:
            tok = (eng.sem, eng.count + 1, eng.name)
        eng.ops.append((waits, fn, eng.sem if inc else None, 1))
        self._record(tok, reads, writes)

    def dma(self, engname, fn, sem, reads=(), writes=(), group=1, idx=0):
        eng = self.engs[engname]
        k = id(sem)
        issued = self.dma_issued.get(k, 0)
        waits = self._deps(eng, reads, writes, cap=(k, issued))
        self.dma_issued[k] = issued + 16
        if idx == 0:
            self.dma_counts[k] = self.dma_counts.get(k, 0) + 16 * group
        tok = (sem, self.dma_counts[k], "dma:%d" % k)
        eng.ops.append((waits, fn, sem, 16))
        self._record(tok, reads, writes)

    def wait_all(self, engname, keys):
        eng = self.engs[engname]
        waits = self._deps(eng, list(keys), [])
        eng.ops.append((waits, None, None, 0))

    def replay(self, engname, handle):
        for waits, fn, inc_sem, inc_val in self.engs[engname].ops:
            for sem, val in waits:
                handle.wait_ge(sem, val)
            if fn is None:
                continue
            ins = fn(handle)
            if inc_sem is not None:
                ins.then_inc(inc_sem, inc_val)


class Buf:
    def __init__(self, region_name, region_ap_f32, off_bytes, shape, dtype):
        isz = 2 if dtype == BF16 else 4
        n = 1
        for s in shape:
            n *= s
        self.nbytes = n * isz
        self.shape = shape
        self.region = region_name
        self.off = off_bytes
        assert off_bytes % 4 == 0 and self.nbytes % 4 == 0
        v = region_ap_f32[:, off_bytes // 4:(off_bytes + self.nbytes) // 4]
        if dtype == BF16:
            v = v.bitcast(BF16)
        if len(shape) == 2:
            v = v.rearrange("p (a b) -> p a b", b=shape[1])
        elif len(shape) == 3:
            v = v.rearrange("p (a b c) -> p a b c", b=shape[1], c=shape[2])
        self.ap = v

    def keys(self, lo=0, hi=None):
        hi = self.nbytes if hi is None else hi
        lo += self.off
        hi += self.off
        return [(self.region, p) for p in range(lo // PAGE, (hi - 1) // PAGE + 1)]

    def ck(self, i, n=1):
        cb = self.nbytes // self.shape[0]
        return self.keys(i * cb, (i + n) * cb)


def _panel_catalog():
    cat = []

    def add(name, src, k0c, nkc, c0, ncols=512):
        cat.append((name, src, k0c, nkc, c0, ncols))

    for i in range(2):
        add("q%d" % i, "w_in", 0, 16, OFF_Q + 512 * i)
        add("k%d" % i, "w_in", 0, 16, OFF_K + 512 * i)
    for i in range(4):
        add("v%d" % i, "w_in", 0, 16, OFF_V + 512 * i)
    for i in range(4):
        add("r%d" % i, "w_in", 0, 16, OFF_R + 512 * i)
    for i in range(2):
        add("p%d" % i, "w_in", 0, 16, OFF_P + 512 * i)
    for i in range(8):
        add("g%d" % i, "w_in", 0, 16, OFF_GATE + 512 * i)
    for i in range(4):
        add("wa%d" % i, "w_branch_a", 0, 8, 512 * i)
        add("wb%d" % i, "w_branch_b", 0, 16, 512 * i)
    for i in range(4):
        add("wo%d" % i, "w_out", 0, 16, 512 * i)
    for i in range(11):
        add("fg%d" % i, "w_ffn_gate", 0, 16, 512 * i)
        add("fu%d" % i, "w_ffn_up", 0, 16, 512 * i)
    for n in range(4):
        for kp in range(3):
            add("fd%d_%d" % (n, kp), "w_ffn_down", 16 * kp, 16 if kp < 2 else 12, 512 * n)
    return cat


def _tile_panels(kind):
    if kind == "prefix":
        return ["k0", "k1", "v0", "v1", "v2", "v3"]
    if kind == "prefix_last":
        return ["k0", "k1", "v0", "v1", "v2", "v3", "p0", "p1"]
    seq = ["q0", "k0", "q1", "k1", "v0", "v1", "v2", "v3", "r0", "r1", "r2", "r3", "p0", "p1"]
    seq += ["g%d" % i for i in range(8)]
    for i in range(4):
        seq += ["wa%d" % i, "wb%d" % i]
    seq += ["wo%d" % i for i in range(4)]
    for i in range(11):
        seq += ["fg%d" % i, "fu%d" % i]
    for n in range(4):
        seq += ["fd%d_%d" % (n, kp) for kp in range(3)]
    return seq


def build_program(n_prefix=NT, n_main=NT):
    nc = bass.Bass("TRN2", target_bir_lowering=False)
    dram = {}

    def din(name, shape, dt=F32):
        dram[name] = nc.dram_tensor(name, list(shape), dt, kind="ExternalInput").ap()
        return dram[name]

    x_main = din("x_main", [TOK, D])
    x_prev = din("x_prev", [TOK, D])
    din("w_in", [D, DIN])
    wgu_d = din("wgu_aug", [32, 1024])
    wpool_d = din("w_pool", [4, 256, 256])
    din("w_branch_a", [1024, D])
    din("w_branch_b", [D, D])
    din("w_out", [D, D])
    din("w_ffn_gate", [D, DFF])
    din("w_ffn_up", [D, DFF])
    din("w_ffn_down", [DFF, D])
    colv_d = din("colv", [128, NCOLV])
    gbc_d = din("gbc", [2, 128, D])
    ident_d = din("ident", [128, 128], BF16)
    maskT_d = din("maskT", [128, 128], BF16)
    tri_d = din("tri", [128, 128])
    invc_d = din("invc", [2, 128, 8, 16])
    out_d = nc.dram_tensor("out", [TOK, D], F32, kind="ExternalOutput").ap()
    cat = _panel_catalog()
    pidx = {c[0]: i for i, c in enumerate(cat)}
    wscr = nc.dram_tensor("wscr", [len(cat), 128, 16 * 512], BF16, kind="Internal").ap()

    es = ExitStack()
    with es:
        def sbuf(name, nfloats):
            return es.enter_context(nc.sbuf_tensor(name, [128, nfloats], F32))

        def sem(name):
            return es.enter_context(nc.semaphore(name))

        S = Sched()
        for n_, pe_ in (("pe", True), ("act", False), ("dve", False), ("pool", False), ("sp", False)):
            S.add_engine(n_, sem("s_" + n_), is_pe=pe_)

        def region(name, nbytes):
            return (name, sbuf(name, nbytes // 4)[:])

        KB = 1024
        rP = region("rP", 34 * KB)
        rX = region("rX", 16 * KB)
        rH = region("rH", 16 * KB)
        rA = region("rA", 28 * KB)
        rB = region("rB", 24 * KB)
        rC = region("rC", 10 * KB)
        rW = region("rW", NSLOT * 16 * KB)
        rT = region("rT", 11 * KB)

        def mk(reg, off, shape, dt):
            return Buf(reg[0], reg[1], off, shape, dt)

        Shat = mk(rP, 0, [8, 512], F32)
        Sbf = mk(rP, 16 * KB, [8, 512], BF16)
        pbuf = mk(rP, 24 * KB, [8, 16 + TT], F32)
        gbc = mk(rA, 16 * KB, [2048], F32)
        Dall = mk(rP, 24 * KB + 8704, [8, 4], F32)
        xres = mk(rX, 0, [NJ, D], F32)
        hT = mk(rH, 0, [16, TT], BF16)
        oT = mk(rH, 8 * KB, [16, TT], BF16)
        la = mk(rA, 0, [NJ, 1024], F32)
        qT = mk(rA, 8 * KB, [8, TT], BF16)
        kT = mk(rA, 12 * KB, [8, TT], BF16)
        ktm = mk(rA, 16 * KB, [NJ, 1024], BF16)
        vtm = mk(rA, 20 * KB, [NJ, D], BF16)
        gates = mk(rA, 0, [32, TT], BF16)
        mtm = mk(rA, 0, [NJ, D], F32)
        xs = mk(rB, 0, [NJ, D], BF16)
        rs = mk(rB, 8 * KB, [16, TT], BF16)
        ptmp0 = mk(rB, 0, [2, 16 + TT], F32)
        ptmp1 = mk(rB, 2176, [2, 16 + TT], F32)
        dT = mk(rB, 4352 + 256, [8, TT], BF16)
        ypT = mk(rB, 4608 + 4 * KB, [8, TT], BF16)
        mixT = mk(rB, 16 * KB, [16, TT], BF16)
        aT = mk(rB, 0, [44, TT], BF16)
        ident = mk(rC, 0, [128], BF16)
        maskT = mk(rC, 256, [128], BF16)
        tri = mk(rC, 512, [128], F32)
        colv = mk(rC, 1024, [NCOLV], F32)
        wg = mk(rC, 1536, [16, 16], BF16)
        wgu = mk(rC, 2048, [1024], BF16)
        wpool = mk(rC, 4096, [4, 2, 256], BF16)
        glr = mk(rC, 8192, [TT], BF16)
        invc = mk(rC, 8704, [2, 8, 16], F32)
        stat = mk(rC, 9728, [64], F32)
        junk = mk(rB, 16 * KB, [2048], BF16)
        eq = [mk(rT, 0 * KB + i * KB, [TT], F32) for i in range(2)]
        ek = [mk(rT, 2 * KB + i * KB, [TT], F32) for i in range(2)]
        etmp = [mk(rT, 4 * KB + i * 2 * KB, [512], F32) for i in range(2)]
        atm = [mk(rT, 8 * KB + i * 256, [128], BF16) for i in range(2)]
        osb = [mk(rT, 8 * KB + 512 + i * KB, [512], BF16) for i in range(2)]
        slots = [mk(rW, i * 16 * KB, [16, 512], BF16) for i in range(NSLOT)]

        psf = [es.enter_context(nc.psum_tensor("psf%d" % i, [128, 512], F32)) for i in range(6)]
        psb = [es.enter_context(nc.psum_tensor("psb%d" % i, [128, 1024], BF16)) for i in range(2)]
        bank_ctr = [0, 0]

        def nbf():
            b = bank_ctr[0] % 6
            bank_ctr[0] += 1
            return psf[b], [("psf", b)]

        def nbb():
            b = bank_ctr[1] % 2
            bank_ctr[1] += 1
            return psb[b], [("psb", b)]

        s_const = sem("s_const")
        s_slot = [sem("s_slot%d" % i) for i in range(NSLOT)]
        s_x = sem("s_x")
        s_out = sem("s_out")
        s_gbc = sem("s_gbc")

        consts = [
            (ident, ident_d, "sp"), (maskT, maskT_d, "sp"), (tri, tri_d, "sp"),
            (colv, colv_d, "sp"),
        ]
        nconst = len(consts) + 1
        for i, (b, src, q) in enumerate(consts):
            S.dma("sp", lambda e, b=b, src=src: e.dma_start(out=b.ap, in_=src), s_const,
                  writes=b.keys(), group=nconst, idx=i)
        S.dma("sp", lambda e: e.dma_start(out=invc.ap, in_=invc_d.rearrange("a p c t -> p a c t")),
              s_const, writes=invc.keys(), group=nconst, idx=nconst - 1)
        s_small = sem("s_small")
        w_in_d = dram["w_in"]
        st_wg = mk(rT, 4 * KB, [16, 16], F32)
        st_wgu = mk(rB, 16 * KB, [1024], F32)
        st_wp = mk(rA, 0, [4, 2, 256], F32)
        wg_src = w_in_d[:, OFF_G:OFF_G + 16].rearrange("(kc p) c -> p kc c", p=128)
        for i in range(8):
            S.dma("sp", lambda e, i=i: e.dma_start(out=st_wg.ap[:, 2 * i:2 * i + 2, :], in_=wg_src[:, 2 * i:2 * i + 2, :]),
                  s_small, writes=st_wg.ck(2 * i, 2), group=13, idx=i)
        S.dma("sp", lambda e: e.dma_start(out=st_wgu.ap[0:32, :], in_=wgu_d), s_small, writes=st_wgu.keys(), group=13, idx=8)
        wp_src = wpool_d.rearrange("g (kc p) d -> p g kc d", p=128)
        for g in range(4):
            S.dma("sp", lambda e, g=g: e.dma_start(out=st_wp.ap[:, g, :, :], in_=wp_src[:, g, :, :]),
                  s_small, writes=st_wp.ck(g), group=13, idx=9 + g)
        S.op("dve", lambda e: e.tensor_copy(out=wg.ap, in_=st_wg.ap), reads=st_wg.keys(), writes=wg.keys())
        S.op("dve", lambda e: e.tensor_copy(out=wgu.ap[0:32, :], in_=st_wgu.ap[0:32, :]), reads=st_wgu.keys(), writes=wgu.keys())
        S.op("dve", lambda e: e.tensor_copy(out=wpool.ap, in_=st_wp.ap), reads=st_wp.keys(), writes=wpool.keys())
        catd = {c[0]: c for c in cat}
        stf = [mk(rW, 0, [16, 512], F32), mk(rW, 32 * KB, [16, 512], F32)]
        stb = [mk(rB, 0, [16, 512], BF16), mk(rH, 0, [16, 512], BF16)]
        s_ld = [sem("s_ld0"), sem("s_ld1")]
        s_st = [sem("s_st0"), sem("s_st1")]
        early = ["k0", "k1", "v0", "v1", "v2", "v3", "p0", "p1"] if n_prefix else [c[0] for c in cat]
        for n, name in enumerate(early):
            _, src, k0c, nkc, c0, ncols = catd[name]
            bb = n % 2
            sap = dram[src][k0c * 128:(k0c + nkc) * 128, c0:c0 + 512].rearrange("(kc p) c -> p kc c", p=128)
            npc = nkc // 2
            for i in range(npc):
                S.dma("sp", lambda e, i=i, bb=bb, sap=sap: e.dma_start(
                    out=stf[bb].ap[:, 2 * i:2 * i + 2, :], in_=sap[:, 2 * i:2 * i + 2, :]),
                    s_ld[bb], writes=stf[bb].ck(2 * i, 2), group=npc, idx=i)
            S.op("dve", lambda e, bb=bb, nkc=nkc: e.tensor_copy(out=stb[bb].ap[:, 0:nkc, :], in_=stf[bb].ap[:, 0:nkc, :]),
                 reads=stf[bb].keys(), writes=stb[bb].keys())
            dap = wscr[pidx[name]].rearrange("p (kc c) -> p kc c", c=512)[:, 0:nkc, :]
            S.dma("act", lambda e, bb=bb, nkc=nkc, dap=dap: e.dma_start(out=dap, in_=stb[bb].ap[:, 0:nkc, :]),
                  s_st[bb], reads=stb[bb].keys(), writes=[("wscr", name, q0) for q0 in range(0, nkc, 4)])
        stfS = mk(rB, 8 * KB, [4, 512], F32)
        stbS = [mk(rH, 8 * KB, [4, 512], BF16), mk(rH, 12 * KB, [4, 512], BF16)]
        s_ldS = sem("s_ldS")
        s_stS = [sem("s_stS0"), sem("s_stS1")]
        conv_pending = []
        seen_p = set(early)
        for name in _tile_panels("main"):
            if name in seen_p:
                continue
            seen_p.add(name)
            for q0 in range(0, catd[name][3], 4):
                conv_pending.append((name, q0))
        qctr = [0]

        def conv_some(k):
            for _ in range(k):
                if not conv_pending:
                    return
                name, q0 = conv_pending.pop(0)
                _, src, k0c, nkc, c0, ncols = catd[name]
                sap = dram[src][(k0c + q0) * 128:(k0c + q0 + 4) * 128, c0:c0 + 512].rearrange("(kc p) c -> p kc c", p=128)
                for i in range(2):
                    S.dma("sp", lambda e, i=i, sap=sap: e.dma_start(
                        out=stfS.ap[:, 2 * i:2 * i + 2, :], in_=sap[:, 2 * i:2 * i + 2, :]),
                        s_ldS, writes=stfS.ck(2 * i, 2), group=2, idx=i)
                bb = qctr[0] % 2
                qctr[0] += 1
                S.op("dve", lambda e, bb=bb: e.tensor_copy(out=stbS[bb].ap, in_=stfS.ap),
                     reads=stfS.keys(), writes=stbS[bb].keys())
                dap = wscr[pidx[name]].rearrange("p (kc c) -> p kc c", c=512)[:, q0:q0 + 4, :]
                S.dma("act", lambda e, bb=bb, dap=dap: e.dma_start(out=dap, in_=stbS[bb].ap),
                      s_stS[bb], reads=stbS[bb].keys(), writes=[("wscr", name, q0)])

        conv_rate = [max(3, -(-len(conv_pending) // max(1, 6 * n_prefix - 4)))]

        def conv_force(name):
            while any(nm == name for nm, _ in conv_pending):
                conv_some(1)
        S.op("dve", lambda e: e.memset(Shat.ap, 0.0), writes=Shat.keys())
        S.op("dve", lambda e: e.memset(Sbf.ap, 0.0), writes=Sbf.keys())
        S.op("dve", lambda e: e.memset(pbuf.ap, 0.0), writes=pbuf.keys())
        S.op("dve", lambda e: e.memset(Dall.ap, 1.0), writes=Dall.keys())
        S.op("dve", lambda e: e.memset(glr.ap[0:32, :], 1.0), writes=glr.keys())

        kinds = ["prefix"] * n_prefix
        if n_prefix:
            kinds[-1] = "prefix_last"
        kinds += ["main"] * n_main
        plist = []
        for kd in kinds:
            plist += _tile_panels(kd)
        pstate = {"cons": 0, "nload": 0}

        def issue_load(n):
            name = plist[n]
            conv_force(name)
            _, src, k0c, nkc, c0, ncols = catd[name]
            s = n % NSLOT
            sl = slots[s]
            S.dma("sp", lambda e, sl=sl, name=name, nkc=nkc: e.dma_start(
                out=sl.ap[:, 0:nkc, :],
                in_=wscr[pidx[name]].rearrange("p (kc c) -> p kc c", c=512)[:, 0:nkc, :]),
                s_slot[s], reads=[("wscr", name, q0) for q0 in range(0, nkc, 4)], writes=sl.keys())

        def acquire(name):
            assert plist[pstate["cons"]] == name, (plist[pstate["cons"]], name)
            while pstate["nload"] < len(plist) and pstate["nload"] < pstate["cons"] + NSLOT - 1:
                issue_load(pstate["nload"])
                pstate["nload"] += 1
            conv_some(conv_rate[0])
            s = pstate["cons"] % NSLOT
            pstate["cons"] += 1
            return slots[s]

        flip = {"ev": 0}

        _EV = os.environ.get("EVAC", "")

        def evac_engine():
            flip["ev"] ^= 1
            if _EV:
                return _EV
            return "dve" if flip["ev"] else "act"

        def copy_op(eng, out_ap, in_ap, reads, writes):
            if eng == "act":
                S.op("act", lambda e: e.activation(out=out_ap, in_=in_ap, func=AF.Copy), reads=reads, writes=writes)
            else:
                S.op(eng, lambda e: e.tensor_copy(out=out_ap, in_=in_ap), reads=reads, writes=writes)

        def scale_op(eng, out_ap, in_ap, scal_ap, reads, writes):
            if eng == "act":
                S.op("act", lambda e: e.activation(out=out_ap, in_=in_ap, func=AF.Copy, scale=scal_ap),
                     reads=reads, writes=writes)
            else:
                S.op(eng, lambda e: e.tensor_scalar_mul(out=out_ap, in0=in_ap, scalar1=scal_ap),
                     reads=reads, writes=writes)

        def mm_group(out_ap, pairs, reads_list, wkeys):
            n = len(pairs)
            for i, (l, r) in enumerate(pairs):
                S.op("pe", lambda e, l=l, r=r, i=i: e.matmul(out_ap, lhsT=l, rhs=r, start=(i == 0), stop=(i == n - 1)),
                     reads=reads_list[i] if isinstance(reads_list[0], list) else reads_list,
                     writes=wkeys, inc=(i == n - 1))

        cv = colv.ap
        st = stat.ap
        stat_k = stat.keys()

        def rstd_from_ms(ms_ap, tmp_ap, out_ap):
            S.op("act", lambda e: e.activation(out=tmp_ap, in_=ms_ap, func=AF.Ln, bias=EPS), reads=stat_k, writes=stat_k)
            S.op("act", lambda e: e.activation(out=out_ap, in_=tmp_ap, func=AF.Exp, scale=-0.5), reads=stat_k, writes=stat_k)

        def stage_norm_T(gcol):
            for j in range(NJ):
                S.op("act", lambda e, j=j: e.activation(out=junk.ap, in_=xres.ap[:, j, :], func=AF.Square,
                                                         scale=RS_D, accum_out=st[:, j:j + 1]),
                     reads=xres.ck(j), writes=junk.keys() + stat_k)
            if _STOPB <= 1:
                return
            rstd_from_ms(st[:, 0:NJ], st[:, 2:2 + NJ], st[:, 4:4 + NJ])
            if _STOPB <= 2:
                return
            for j in range(NJ):
                S.op("dve", lambda e, j=j: e.tensor_scalar_mul(out=xs.ap[:, j, :], in0=xres.ap[:, j, :],
                                                                scalar1=st[:, 4 + j:5 + j]),
                     reads=xres.ck(j) + stat_k, writes=xs.ck(j))
            if _STOPB <= 3:
                return
            for k4 in range(4):
                if _STOPB <= 4 and k4 >= 1:
                    return
                pb, pk = nbb()
                pv = pb[:].rearrange("p (a t) -> p a t", t=TT)
                for kk in range(4):
                    kc = k4 * 4 + kk
                    for j in range(NJ):
                        S.op("pe", lambda e, kk=kk, kc=kc, j=j, pv=pv: e.transpose(
                            out=pv[:, kk, j * 128:(j + 1) * 128], in_=xs.ap[:, j, kc * 128:(kc + 1) * 128],
                            identity=ident.ap),
                            reads=xs.ck(j) + ident.keys(), writes=pk, inc=(kk == 3 and j == NJ - 1))
                ev_eng = evac_engine()
                for kk in range(4):
                    kc = k4 * 4 + kk
                    scale_op(ev_eng, hT.ap[:, kc, :], pv[:, kk, :], cv[:, gcol + kc:gcol + kc + 1],
                             reads=pk + colv.keys(), writes=hT.ck(kc))

        def proj_fm(slot, cc, nkc, rhs_buf, ncol=TT, col0=0):
            pb, pk = nbf()
            out_ap = pb[:, 0:ncol]
            pairs = [(slot.ap[:, kc, cc * 128:(cc + 1) * 128], rhs_buf.ap[:, kc, col0:col0 + ncol]) for kc in range(nkc)]
            reads = [slot.ck(kc) + rhs_buf.ck(kc) for kc in range(nkc)]
            mm_group(out_ap, pairs, reads, pk)
            return out_ap, pk

        gchunk = {"g": 0}

        def load_x(src, t):
            xv = src[t * TT:(t + 1) * TT, :].rearrange("(j p) d -> p j d", p=128)
            S.dma("act", lambda e: e.dma_start(out=xres.ap, in_=xv), s_x, writes=xres.keys())

        tile_seq = []
        xstate = {"i": 0, "loaded": False}

        def tile(kind, t):
            main = kind == "main"
            src = x_main if main else x_prev
            if not xstate["loaded"]:
                load_x(src, t)
            xstate["loaded"] = False
            stage_norm_T(C_GPRE)
            xstate["i"] += 1
            if not main and _STOP > 1 and xstate["i"] < len(tile_seq):
                nk, nt_ = tile_seq[xstate["i"]]
                load_x(x_main if nk == "main" else x_prev, nt_)
                xstate["loaded"] = True
            if _STOP <= 1:
                pstate['cons'] = len(plist); return
            g0 = gchunk["g"]
            gchunk["g"] += NJ
            pb, pk = nbf()
            mm_group(pb[0:16, 0:TT],
                     [(wg.ap[:, kc, :], hT.ap[:, kc, :]) for kc in range(16)],
                     [wg.keys() + hT.ck(kc) for kc in range(16)], pk)
            copy_op("dve", glr.ap[0:16, :], pb[0:16, 0:TT], pk, glr.keys())
            for j in range(NJ):
                for half in range(2):
                    pb, pk = nbf()
                    mm_group(pb[:, 0:512], [(glr.ap[0:32, j * 128:(j + 1) * 128], wgu.ap[0:32, half * 512:(half + 1) * 512])],
                             glr.keys() + wgu.keys(), pk)
                    et = etmp[(j * 2 + half) % 2]
                    S.op("act", lambda e, pb=pb, et=et: e.activation(out=et.ap, in_=pb[:, 0:512], func=AF.Exp, scale=-1.0),
                         reads=pk, writes=et.keys())
                    S.op("act", lambda e, et=et, j=j, half=half: e.activation(
                        out=la.ap[:, j, half * 512:(half + 1) * 512], in_=et.ap, func=AF.Ln, bias=1.0),
                        reads=et.keys(), writes=la.keys(j * 4096 + half * 2048, j * 4096 + (half + 1) * 2048))
            if _STOP <= 2:
                pstate['cons'] = len(plist); return
            def kT_transposes(c):
                pb2, pbk = nbb()
                for j in range(NJ):
                    S.op("pe", lambda e, pb2=pb2, j=j, c=c: e.transpose(
                        out=pb2[:, j * 128:(j + 1) * 128], in_=kT.ap[:, c, j * 128:(j + 1) * 128], identity=ident.ap),
                        reads=kT.ck(c) + ident.keys(), writes=pbk, inc=(j == NJ - 1))
                copy_op("act", ktm.ap[:, :, c * 128:(c + 1) * 128],
                        pb2[:, 0:NJ * 128].rearrange("p (j f) -> p j f", f=128), pbk, ktm.keys())

            slot_q = slot_k = None
            for c in range(8):
                if c % 4 == 0:
                    if main:
                        slot_q = acquire("q%d" % (c // 4))
                    slot_k = acquire("k%d" % (c // 4))
                cc = c % 4
                pg, pgk = nbf()
                for j in range(NJ):
                    S.op("pe", lambda e, pg=pg, j=j, c=c: e.matmul(
                        pg[:, j * 128:(j + 1) * 128], lhsT=la.ap[:, j, c * 128:(c + 1) * 128], rhs=tri.ap,
                        start=True, stop=True),
                        reads=la.keys(j * 4096 + c * 512, j * 4096 + (c + 1) * 512) + tri.keys(), writes=pgk,
                        inc=(j == NJ - 1))
                eqb, ekb = eq[c % 2], ek[c % 2]
                if main:
                    S.op("act", lambda e, pg=pg, eqb=eqb: e.activation(out=eqb.ap, in_=pg[:, 0:TT], func=AF.Exp,
                                                                      bias=LN_QSCALE), reads=pgk, writes=eqb.keys())
                S.op("act", lambda e, pg=pg, ekb=ekb: e.activation(out=ekb.ap, in_=pg[:, 0:TT], func=AF.Exp, scale=-1.0),
                     reads=pgk, writes=ekb.keys())
                for j in range(NJ):
                    dcol = (g0 + j) % 4
                    S.op("act", lambda e, pg=pg, j=j, c=c, dcol=dcol: e.activation(
                        out=Dall.ap[:, c, dcol:dcol + 1], in_=pg[:, j * 128 + 127:j * 128 + 128], func=AF.Exp),
                        reads=pgk, writes=Dall.keys())
                if main:
                    po, pok = proj_fm(slot_q, cc, 16, hT)
                    S.op("dve", lambda e, po=po, eqb=eqb, c=c: e.tensor_tensor(out=qT.ap[:, c, :], in0=po, in1=eqb.ap,
                                                                            op=ALU.mult),
                         reads=pok + eqb.keys(), writes=qT.ck(c))
                po, pok = proj_fm(slot_k, cc, 16, hT)
                S.op("dve", lambda e, po=po, ekb=ekb, c=c: e.tensor_tensor(out=kT.ap[:, c, :], in0=po, in1=ekb.ap,
                                                                        op=ALU.mult),
                     reads=pok + ekb.keys(), writes=kT.ck(c))
                if c >= 1:
                    kT_transposes(c - 1)
            kT_transposes(7)
            if _STOP <= 3:
                pstate['cons'] = len(plist); return
            for n in range(4):
                sl = acquire("v%d" % n)
                for j in range(NJ):
                    pb, pk = nbf()
                    mm_group(pb[:, 0:512],
                             [(hT.ap[:, kc, j * 128:(j + 1) * 128], sl.ap[:, kc, :]) for kc in range(16)],
                             [hT.ck(kc) + sl.ck(kc) for kc in range(16)], pk)
                    copy_op(evac_engine(), vtm.ap[:, j, n * 512:(n + 1) * 512], pb[:, 0:512], pk,
                            vtm.keys(j * 4096 + n * 1024, j * 4096 + (n + 1) * 1024))
            if _STOP <= 4:
                pstate['cons'] = len(plist); return
            if main:
                for n in range(4):
                    sl = acquire("r%d" % n)
                    for cc in range(4):
                        fc = n * 4 + cc
                        po, pok = proj_fm(sl, cc, 16, hT)
                        et = etmp[fc % 2]
                        S.op("act", lambda e, po=po, et=et: e.activation(out=et.ap[:, 0:TT], in_=po, func=AF.Silu),
                             reads=pok, writes=et.keys())
                        S.op("dve", lambda e, et=et, fc=fc: e.tensor_scalar_mul(
                            out=rs.ap[:, fc, :], in0=et.ap[:, 0:TT], scalar1=cv[:, C_GN + fc % 4:C_GN + fc % 4 + 1]),
                            reads=et.keys() + colv.keys(), writes=rs.ck(fc))
            items = [(j, h) for j in range(NJ) for h in range(4)]
            ctx = {}

            def fixed_bank(b):
                return psf[b], [("psf", b)]

            def part1(i):
                j, h = items[i]
                js = slice(j * 128, (j + 1) * 128)
                pa, pak = fixed_bank(i % 2)
                mm_group(pa[:, 0:128],
                         [(kT.ap[:, 2 * h + kk, js], qT.ap[:, 2 * h + kk, js]) for kk in range(2)],
                         [kT.ck(2 * h + kk) + qT.ck(2 * h + kk) for kk in range(2)], pak)
                am = atm[i % 2]
                S.op("dve", lambda e, pa=pa, am=am: e.tensor_tensor(out=am.ap, in0=pa[:, 0:128], in1=maskT.ap,
                                                                 op=ALU.mult),
                     reads=pak + maskT.keys(), writes=am.keys())

            def part2(i):
                j, h = items[i]
                gi = g0 + j
                dcur, dprev = gi % 4, (gi - 1) % 4
                js = slice(j * 128, (j + 1) * 128)
                vslice = vtm.ap[:, j, h * 512:(h + 1) * 512]
                vkeys = vtm.keys(j * 4096 + h * 1024, j * 4096 + (h + 1) * 1024)
                if main:
                    am = atm[i % 2]
                    po, pok = fixed_bank(2 + i % 2)
                    mm_group(po[:, 0:512],
                             [(qT.ap[:, 2 * h, js], Sbf.ap[:, 2 * h, :]),
                              (qT.ap[:, 2 * h + 1, js], Sbf.ap[:, 2 * h + 1, :]),
                              (am.ap, vslice)],
                             [qT.ck(2 * h) + Sbf.ck(2 * h), qT.ck(2 * h + 1) + Sbf.ck(2 * h + 1), am.keys() + vkeys],
                             pok)
                for kk in range(2):
                    fc = 2 * h + kk
                    ps_, psk = fixed_bank(4 + kk)
                    mm_group(ps_[:, 0:512], [(ktm.ap[:, j, fc * 128:(fc + 1) * 128], vslice)],
                             ktm.keys(j * 2048 + fc * 256, j * 2048 + (fc + 1) * 256) + vkeys, psk)
                    S.op("dve", lambda e, ps_=ps_, fc=fc, dprev=dprev: e.scalar_tensor_tensor(
                        out=Shat.ap[:, fc, :], in0=Shat.ap[:, fc, :], scalar=Dall.ap[:, fc, dprev:dprev + 1],
                        in1=ps_[:, 0:512], op0=ALU.mult, op1=ALU.add),
                        reads=psk + Shat.ck(fc) + Dall.keys(), writes=Shat.ck(fc))
                    S.op("act", lambda e, fc=fc, dcur=dcur: e.activation(
                        out=Sbf.ap[:, fc, :], in_=Shat.ap[:, fc, :], func=AF.Copy, scale=Dall.ap[:, fc, dcur:dcur + 1]),
                        reads=Shat.ck(fc) + Dall.keys(), writes=Sbf.ck(fc))
                if main:
                    si = 8 + i
                    S.op("act", lambda e, po=po, si=si: e.activation(out=junk.ap[:, 0:512], in_=po[:, 0:512],
                                                                  func=AF.Square, scale=RS_V,
                                                                  accum_out=st[:, si:si + 1]),
                         reads=pok, writes=junk.keys() + stat_k)
                    rstd_from_ms(st[:, si:si + 1], st[:, si + 16:si + 17], st[:, si + 32:si + 33])
                    ob = osb[i % 2]
                    S.op("dve", lambda e, po=po, ob=ob, si=si: e.tensor_scalar_mul(
                        out=ob.ap, in0=po[:, 0:512], scalar1=st[:, si + 32:si + 33]),
                        reads=pok + stat_k, writes=ob.keys())

            def part3(i):
                j, h = items[i]
                js = slice(j * 128, (j + 1) * 128)
                ob = osb[i % 2]
                pb2, pbk = nbb()
                for qd in range(4):
                    S.op("pe", lambda e, pb2=pb2, qd=qd, ob=ob: e.transpose(
                        out=pb2[:, qd * 128:(qd + 1) * 128], in_=ob.ap[:, qd * 128:(qd + 1) * 128],
                        identity=ident.ap),
                        reads=ob.keys() + ident.keys(), writes=pbk, inc=(qd == 3))
                S.op("dve", lambda e, pb2=pb2, h=h, js=js: e.tensor_tensor(
                    out=oT.ap[:, 4 * h:4 * h + 4, js], in0=pb2[:, 0:512].rearrange("p (a t) -> p a t", t=128),
                    in1=rs.ap[:, 4 * h:4 * h + 4, js], op=ALU.mult),
                    reads=pbk + rs.ck(4 * h, 4), writes=oT.ck(4 * h, 4))

            if main:
                part1(0)
            for i in range(len(items)):
                if main and i + 1 < len(items):
                    part1(i + 1)
                part2(i)
                if main and i >= 1:
                    part3(i - 1)
            if main:
                part3(len(items) - 1)
            if _STOP <= 5:
                pstate['cons'] = len(plist); return
            if kind == "prefix":
                return
            for n in range(2):
                sl = acquire("p%d" % n)
                for cc in range(4):
                    fc = n * 4 + cc
                    if main:
                        po, pok = proj_fm(sl, cc, 16, hT)
                        copy_op(evac_engine(), pbuf.ap[:, fc, 16:16 + TT], po, pok, pbuf.ck(fc))
                    else:
                        po, pok = proj_fm(sl, cc, 16, hT, ncol=16, col0=TT - 16)
                        copy_op(evac_engine(), pbuf.ap[:, fc, 0:16], po, pok, pbuf.ck(fc))
            if not main:
                return
            first = (t == 0)
            W = 16 + TT
            for g in range(4):
                w = 2 << g
                srcp = pbuf.ap[:, 2 * g:2 * g + 2, :]
                skeys = pbuf.ck(2 * g, 2)
                cur, curk = srcp, skeys
                bufs = [ptmp0, ptmp1]
                sh = 1
                bi = 0
                while sh < w:
                    ob_ = bufs[bi]
                    S.op("pool", lambda e, ob_=ob_, cur=cur, sh=sh: e.tensor_tensor(
                        out=ob_.ap[:, :, 2 * sh - 1:W], in0=cur[:, :, 2 * sh - 1:W], in1=cur[:, :, sh - 1:W - sh],
                        op=ALU.add), reads=curk, writes=ob_.keys())
                    cur, curk = ob_.ap, ob_.keys()
                    sh *= 2
                    bi ^= 1
                S.op("dve", lambda e, cur=cur, srcp=srcp, g=g, w=w: e.scalar_tensor_tensor(
                    out=dT.ap[:, 2 * g:2 * g + 2, :], in0=cur[:, :, 16:W], scalar=1.0 / w, in1=srcp[:, :, 16:W],
                    op0=ALU.mult, op1=ALU.subtract), reads=curk + skeys, writes=dT.ck(2 * g, 2))
                iv = invc.ap[:, 0 if first else 1, 2 * g:2 * g + 2, :]
                other = bufs[bi]
                S.op("pool", lambda e, cur=cur, iv=iv, other=other: e.tensor_tensor(
                    out=other.ap[:, :, 0:16], in0=cur[:, :, 16:32], in1=iv, op=ALU.mult),
                    reads=curk + invc.keys(), writes=other.keys())
                S.op("pool", lambda e, other=other, srcp=srcp, g=g: e.tensor_tensor(
                    out=dT.ap[:, 2 * g:2 * g + 2, 0:16], in0=other.ap[:, :, 0:16], in1=srcp[:, :, 16:32],
                    op=ALU.subtract), reads=other.keys() + skeys, writes=dT.ck(2 * g, 2))
            S.op("pool", lambda e: e.tensor_copy(out=pbuf.ap[:, :, 0:16], in_=pbuf.ap[:, :, TT:TT + 16]),
                 reads=pbuf.keys(), writes=pbuf.keys())
            for n in range(8):
                sl = acquire("g%d" % n)
                for cc in range(4):
                    fc = n * 4 + cc
                    po, pok = proj_fm(sl, cc, 16, hT)
                    S.op("act", lambda e, po=po, fc=fc: e.activation(
                        out=gates.ap[:, fc, :], in_=po, func=AF.Sigmoid, bias=cv[:, C_BG + fc:C_BG + fc + 1]),
                        reads=pok + colv.keys(), writes=gates.ck(fc))
            for g in range(4):
                for m in range(2):
                    pb, pk = nbf()
                    mm_group(pb[:, 0:TT],
                             [(wpool.ap[:, g, kc, m * 128:(m + 1) * 128], dT.ap[:, 2 * g + kc, :]) for kc in range(2)],
                             [wpool.keys() + dT.ck(2 * g + kc) for kc in range(2)], pk)
                    fc = 2 * g + m
                    scale_op(evac_engine(), ypT.ap[:, fc, :], pb[:, 0:TT], cv[:, C_PSC + fc:C_PSC + fc + 1],
                             reads=pk + colv.keys(), writes=ypT.ck(fc))
            for n in range(4):
                sla = acquire("wa%d" % n)
                slb = acquire("wb%d" % n)
                for cc in range(4):
                    m = n * 4 + cc
                    pa_, pak = proj_fm(sla, cc, 8, ypT)
                    ta = etmp[0]
                    S.op("dve", lambda e, pa_=pa_, ta=ta, m=m: e.tensor_tensor(
                        out=ta.ap[:, 0:TT], in0=pa_, in1=gates.ap[:, m, :], op=ALU.mult),
                        reads=pak + gates.ck(m), writes=ta.keys())
                    pb_, pbk_ = proj_fm(slb, cc, 16, oT)
                    tb = etmp[1]
                    S.op("dve", lambda e, pb_=pb_, tb=tb, m=m: e.tensor_tensor(
                        out=tb.ap[:, 0:TT], in0=pb_, in1=gates.ap[:, 16 + m, :], op=ALU.mult),
                        reads=pbk_ + gates.ck(16 + m), writes=tb.keys())
                    S.op("dve", lambda e, ta=ta, tb=tb, m=m: e.tensor_tensor(
                        out=mixT.ap[:, m, :], in0=ta.ap[:, 0:TT], in1=tb.ap[:, 0:TT], op=ALU.add),
                        reads=ta.keys() + tb.keys(), writes=mixT.ck(m))
            S.dma("act", lambda e: e.dma_start(out=gbc.ap, in_=gbc_d[0]), s_gbc, writes=gbc.keys())
            for n in range(4):
                sl = acquire("wo%d" % n)
                for j in range(NJ):
                    pb, pk = nbf()
                    mm_group(pb[:, 0:512],
                             [(mixT.ap[:, kc, j * 128:(j + 1) * 128], sl.ap[:, kc, :]) for kc in range(16)],
                             [mixT.ck(kc) + sl.ck(kc) for kc in range(16)], pk)
                    copy_op(evac_engine(), mtm.ap[:, j, n * 512:(n + 1) * 512], pb[:, 0:512], pk,
                            mtm.keys(j * 8192 + n * 2048, j * 8192 + (n + 1) * 2048))
            post_norm_residual()
            stage_norm_T(C_GFFN)
            for i in range(11):
                slg = acquire("fg%d" % i)
                slu = acquire("fu%d" % i)
                for cc in range(4):
                    fc = i * 4 + cc
                    pg_, pgk_ = proj_fm(slg, cc, 16, hT)
                    pu_, puk_ = proj_fm(slu, cc, 16, hT)
                    et = etmp[fc % 2]
                    S.op("act", lambda e, pg_=pg_, et=et: e.activation(out=et.ap[:, 0:TT], in_=pg_, func=AF.Silu),
                         reads=pgk_, writes=et.keys())
                    S.op("dve", lambda e, pu_=pu_, et=et, fc=fc: e.tensor_tensor(
                        out=aT.ap[:, fc, :], in0=pu_, in1=et.ap[:, 0:TT], op=ALU.mult),
                        reads=puk_ + et.keys(), writes=aT.ck(fc))
            S.dma("act", lambda e: e.dma_start(out=gbc.ap, in_=gbc_d[1]), s_gbc, writes=gbc.keys())
            for n in range(4):
                banks = [nbf() for _ in range(NJ)]
                for kp in range(3):
                    sl = acquire("fd%d_%d" % (n, kp))
                    nkc = 16 if kp < 2 else 12
                    for j in range(NJ):
                        pb, pk = banks[j]
                        for kc in range(nkc):
                            f = kp * 16 + kc
                            S.op("pe", lambda e, pb=pb, f=f, j=j, kc=kc, sl=sl, kp=kp, nkc=nkc: e.matmul(
                                pb[:, 0:512], lhsT=aT.ap[:, f, j * 128:(j + 1) * 128], rhs=sl.ap[:, kc, :],
                                start=(f == 0), stop=(f == 43)),
                                reads=aT.ck(f) + sl.ck(kc), writes=pk, inc=(kc == nkc - 1))
                for j in range(NJ):
                    pb, pk = banks[j]
                    copy_op(evac_engine(), mtm.ap[:, j, n * 512:(n + 1) * 512], pb[:, 0:512], pk,
                            mtm.keys(j * 8192 + n * 2048, j * 8192 + (n + 1) * 2048))
            post_norm_residual(final=True)
            ov = out_d[t * TT:(t + 1) * TT, :].rearrange("(j p) d -> p j d", p=128)
            S.dma("act", lambda e: e.dma_start(out=ov, in_=mtm.ap), s_out, reads=mtm.keys(), writes=[("out", t)])

        def post_norm_residual(final=False):
            for j in range(NJ):
                S.op("act", lambda e, j=j: e.activation(out=junk.ap, in_=mtm.ap[:, j, :], func=AF.Square,
                                                         scale=RS_D, accum_out=st[:, j:j + 1]),
                     reads=mtm.ck(j), writes=junk.keys() + stat_k)
            rstd_from_ms(st[:, 0:NJ], st[:, 2:2 + NJ], st[:, 4:4 + NJ])
            for j in range(NJ):
                S.op("dve", lambda e, j=j: e.scalar_tensor_tensor(
                    out=mtm.ap[:, j, :], in0=mtm.ap[:, j, :], scalar=st[:, 4 + j:5 + j], in1=gbc.ap,
                    op0=ALU.mult, op1=ALU.mult), reads=mtm.ck(j) + stat_k + gbc.keys(), writes=mtm.ck(j))
                if final:
                    S.op("dve", lambda e, j=j: e.tensor_tensor(out=mtm.ap[:, j, :], in0=xres.ap[:, j, :],
                                                                 in1=mtm.ap[:, j, :], op=ALU.add),
                         reads=mtm.ck(j) + xres.ck(j), writes=mtm.ck(j))
                else:
                    S.op("dve", lambda e, j=j: e.tensor_tensor(out=xres.ap[:, j, :], in0=xres.ap[:, j, :],
                                                                 in1=mtm.ap[:, j, :], op=ALU.add),
                         reads=mtm.ck(j) + xres.ck(j), writes=xres.ck(j))

        tp = 0
        for kd in kinds:
            if kd != "main":
                tile_seq.append((kd, NT - n_prefix + tp))
                tp += 1
        for t in range(n_main):
            tile_seq.append(("main", t))
        tp = 0
        for kd in kinds:
            if kd == "main":
                break
            tile(kd, NT - n_prefix + tp)
            tp += 1
        conv_some(len(conv_pending))
        for t in range(n_main):
            tile("main", t)
        S.wait_all("act", [("out", t) for t in range(n_main)])
        if n_main == 0:
            S.wait_all("sp", wg.keys() + wgu.keys() + wpool.keys() + invc.keys() + colv.keys())
        assert pstate["cons"] == len(plist)

        with nc.Block() as block:
            @block.tensor
            def _(e):
                S.replay("pe", e)

            @block.scalar
            def _(e):
                S.replay("act", e)

            @block.vector
            def _(e):
                S.replay("dve", e)

            @block.gpsimd
            def _(e):
                S.replay("pool", e)

            @block.sync
            def _(e):
                S.replay("sp", e)
        info = {n: (len(e.ops), sum(len(o[0]) for o in e.ops)) for n, e in S.engs.items()}
        _DBG['S'] = S
        _DBG['bufs'] = {k_: v_ for k_, v_ in locals().items() if isinstance(v_, Buf)}
    return nc, info


def _host_constants():
    ident = np.eye(128, dtype=np.float32).astype(ml_dtypes.bfloat16)
    r = np.arange(128)
    maskT = (r[:, None] <= r[None, :]).astype(np.float32).astype(ml_dtypes.bfloat16)
    tri = np.where(r[:, None] <= r[None, :], np.float32(-1.0 / 16.0), np.float32(0.0)).astype(np.float32)
    invc = np.zeros((2, 128, 8, 16), np.float32)
    for c in range(8):
        w = 2 << (c // 2)
        invc[1, :, c, :] = 1.0 / w
        for tpos in range(16):
            invc[0, :, c, tpos] = 1.0 / min(tpos + 1, w)
    return ident, maskT, tri, invc


def _colvecs(norm_mix_pre, pool_scale, gla_norm, b_branch_gates, norm_ffn_pre):
    cvt = np.zeros((128, NCOLV), np.float32)
    cvt[:, C_GPRE:C_GPRE + 16] = norm_mix_pre.reshape(16, 128).T
    cvt[:, C_PSC:C_PSC + 8] = pool_scale.reshape(8, 128).T
    cvt[:, C_GN:C_GN + 4] = gla_norm.reshape(4, 128).T
    cvt[:, C_BG:C_BG + 32] = b_branch_gates.reshape(32, 128).T
    cvt[:, C_GFFN:C_GFFN + 16] = norm_ffn_pre.reshape(16, 128).T
    return cvt


_CACHE = {}


def prepare_inputs(x, norm_mix_pre, w_in, w_gate_up, b_gate, w_pool, pool_scale, gla_norm,
                   w_branch_a, w_branch_b, b_branch_gates, w_out, norm_mix_post,
                   norm_ffn_pre, w_ffn_gate, w_ffn_up, w_ffn_down, norm_ffn_post):
    f = lambda a: np.ascontiguousarray(np.asarray(a, dtype=np.float32))
    x = f(x)
    ident, maskT, tri, invc = _host_constants()
    invc_rest = np.ascontiguousarray(np.stack([invc[1], invc[1]]))
    wgu = np.zeros((32, 1024), np.float32)
    wgu[0:16] = f(w_gate_up)[0]
    wgu[16] = f(b_gate)[0]
    gbc = np.ascontiguousarray(np.stack([np.broadcast_to(f(norm_mix_post)[0], (128, D)),
                                         np.broadcast_to(f(norm_ffn_post)[0], (128, D))]))
    shared = {
        "w_in": f(w_in)[0], "wgu_aug": wgu, "w_pool": f(w_pool)[0],
        "w_branch_a": f(w_branch_a)[0], "w_branch_b": f(w_branch_b)[0], "w_out": f(w_out)[0],
        "w_ffn_gate": f(w_ffn_gate)[0], "w_ffn_up": f(w_ffn_up)[0], "w_ffn_down": f(w_ffn_down)[0],
        "colv": _colvecs(f(norm_mix_pre)[0], f(pool_scale)[0], f(gla_norm)[0], f(b_branch_gates)[0],
                         f(norm_ffn_pre)[0]),
        "gbc": gbc, "ident": ident, "maskT": maskT, "tri": tri,
    }
    zeros = np.zeros((TOK, D), np.float32)
    in_maps = []
    for c in range(NCORE):
        b, half = c // 2, c % 2
        m = dict(shared)
        m["x_main"] = np.ascontiguousarray(x[b, half * TOK:(half + 1) * TOK])
        m["x_prev"] = np.ascontiguousarray(x[b, 0:TOK]) if half == 1 else zeros
        m["invc"] = invc if half == 0 else invc_rest
        in_maps.append(m)
    return in_maps


def kernel(**inputs):
    if "nc" not in _CACHE:
        _CACHE["nc"] = build_program()[0]
    nc = _CACHE["nc"]
    in_maps = prepare_inputs(**inputs)
    res = run_bass_kernel_spmd(nc, in_maps, core_ids=list(range(NCORE)))
    out = np.empty((BATCH, SEQ, D), np.float32)
    for c in range(NCORE):
        b, half = c // 2, c % 2
        out[b, half * TOK:(half + 1) * TOK] = res.results[c]["out"]
    return out
```

```python
import math
import os
_STOP = int(os.environ.get('STOPAT', '99'))
_DBG = {}
_STOPB = int(os.environ.get('STOPB', '99'))
from contextlib import ExitStack

import numpy as np
import ml_dtypes
import concourse.bass as bass
import concourse.mybir as mybir
from concourse.bass_utils import run_bass_kernel_spmd

F32 = mybir.dt.float32
BF16 = mybir.dt.bfloat16
AF = mybir.ActivationFunctionType
ALU = mybir.AluOpType

D = 2048
SEQ = 8192
BATCH = 4
NCORE = 8
TOK = 4096
TT = 256
NJ = TT // 128
NT = TOK // TT
DIN = 11280
DFF = 5632
EPS = 1e-6
PAGE = 512
NSLOT = 4
OFF_P, OFF_Q, OFF_K, OFF_V, OFF_G, OFF_R, OFF_GATE = 0, 1024, 2048, 3072, 5120, 5136, 7184
RS_D = 1.0 / math.sqrt(2048.0)
RS_V = 1.0 / math.sqrt(512.0)
LN_QSCALE = math.log(1.0 / 16.0)

C_GPRE, C_PSC, C_GN, C_BG, C_GFFN, NCOLV = 0, 16, 24, 28, 60, 76


class _Eng:
    def __init__(self, name, sem, is_pe=False):
        self.name = name
        self.sem = sem
        self.count = 0
        self.ops = []
        self.seen = {}
        self.is_pe = is_pe


class Sched:
    def __init__(self):
        self.engs = {}
        self.state = {}
        self.dma_counts = {}
        self.dma_issued = {}

    def add_engine(self, name, sem, is_pe=False):
        self.engs[name] = _Eng(name, sem, is_pe=is_pe)

    def _deps(self, eng, reads, writes, cap=None):
        need = {}

        def add(tok, kind):
            if tok is None:
                return
            sem, val, ename = tok
            if ename == eng.name:
                if eng.is_pe:
                    return
            k = id(sem)
            if need.get(k, (None, 0))[1] < val:
                need[k] = (sem, val)

        for key in reads:
            st = self.state.get(key)
            if st is not None:
                add(st[0], "raw")
        for key in writes:
            st = self.state.get(key)
            if st is not None:
                add(st[0], "waw")
                for tok in st[1].values():
                    add(tok, "war")
        waits = []
        for k, (sem, val) in need.items():
            if cap is not None and cap[0] == k and val > cap[1]:
                val = cap[1]
                if val <= 0:
                    continue
            if eng.seen.get(k, 0) >= val:
                continue
            eng.seen[k] = val
            waits.append((sem, val))
        return waits

    def _record(self, tok, reads, writes):
        k = id(tok[0])
        for key in reads:
            st = self.state.get(key)
            if st is None:
                st = self.state[key] = [None, {}]
            old = st[1].get(k)
            if old is None or old[1] < tok[1]:
                st[1][k] = tok
        for key in writes:
            self.state[key] = [tok, {}]

    def op(self, engname, fn, reads=(), writes=(), inc=True):
        eng = self.engs[engname]
        waits = self._deps(eng, reads, writes)
        if engname in ("dve", "pool"):
            for sem_, val_ in waits:
                if sem_ is eng.sem and val_ == eng.count:
                    eng.ops.append(([], lambda e: e.engine_nop(), None, 1))
                    break
        if inc:
            eng.count += 1
            tok = (eng.sem, eng.count, eng.name)
        else:
            tok = (eng.sem, eng.count + 1, eng.name)
        eng.ops.append((waits, fn, eng.sem if inc else None, 1))
        self._record(tok, reads, writes)

    def dma(self, engname, fn, sem, reads=(), writes=(), group=1, idx=0):
        eng = self.engs[engname]
        k = id(sem)
        issued = self.dma_issued.get(k, 0)
        waits = self._deps(eng, reads, writes, cap=(k, issued))
        self.dma_issued[k] = issued + 16
        if idx == 0:
            self.dma_counts[k] = self.dma_counts.get(k, 0) + 16 * group
        tok = (sem, self.dma_counts[k], "dma:%d" % k)
        eng.ops.append((waits, fn, sem, 16))
        self._record(tok, reads, writes)

    def wait_all(self, engname, keys):
        eng = self.engs[engname]
        waits = self._deps(eng, list(keys), [])
        eng.ops.append((waits, None, None, 0))

    def replay(self, engname, handle):
        for waits, fn, inc_sem, inc_val in self.engs[engname].ops:
            for sem, val in waits:
                handle.wait_ge(sem, val)
            if fn is None:
                continue
            ins = fn(handle)
            if inc_sem is not None:
                ins.then_inc(inc_sem, inc_val)


class Buf:
    def __init__(self, region_name, region_ap_f32, off_bytes, shape, dtype):
        isz = 2 if dtype == BF16 else 4
        n = 1
        for s in shape:
            n *= s
        self.nbytes = n * isz
        self.shape = shape
        self.region = region_name
        self.off = off_bytes
        assert off_bytes % 4 == 0 and self.nbytes % 4 == 0
        v = region_ap_f32[:, off_bytes // 4:(off_bytes + self.nbytes) // 4]
        if dtype == BF16:
            v = v.bitcast(BF16)
        if len(shape) == 2:
            v = v.rearrange("p (a b) -> p a b", b=shape[1])
        elif len(shape) == 3:
            v = v.rearrange("p (a b c) -> p a b c", b=shape[1], c=shape[2])
        self.ap = v

    def keys(self, lo=0, hi=None):
        hi = self.nbytes if hi is None else hi
        lo += self.off
        hi += self.off
        return [(self.region, p) for p in range(lo // PAGE, (hi - 1) // PAGE + 1)]

    def ck(self, i, n=1):
        cb = self.nbytes // self.shape[0]
        return self.keys(i * cb, (i + n) * cb)


def _panel_catalog():
    cat = []

    def add(name, src, k0c, nkc, c0, ncols=512):
        cat.append((name, src, k0c, nkc, c0, ncols))

    for i in range(2):
        add("q%d" % i, "w_in", 0, 16, OFF_Q + 512 * i)
        add("k%d" % i, "w_in", 0, 16, OFF_K + 512 * i)
    for i in range(4):
        add("v%d" % i, "w_in", 0, 16, OFF_V + 512 * i)
    for i in range(4):
        add("r%d" % i, "w_in", 0, 16, OFF_R + 512 * i)
    for i in range(2):
        add("p%d" % i, "w_in", 0, 16, OFF_P + 512 * i)
    for i in range(8):
        add("g%d" % i, "w_in", 0, 16, OFF_GATE + 512 * i)
    for i in range(4):
        add("wa%d" % i, "w_branch_a", 0, 8, 512 * i)
        add("wb%d" % i, "w_branch_b", 0, 16, 512 * i)
    for i in range(4):
        add("wo%d" % i, "w_out", 0, 16, 512 * i)
    for i in range(11):
        add("fg%d" % i, "w_ffn_gate", 0, 16, 512 * i)
        add("fu%d" % i, "w_ffn_up", 0, 16, 512 * i)
    for n in range(4):
        for kp in range(3):
            add("fd%d_%d" % (n, kp), "w_ffn_down", 16 * kp, 16 if kp < 2 else 12, 512 * n)
    return cat


def _tile_panels(kind):
    if kind == "prefix":
        return ["k0", "k1", "v0", "v1", "v2", "v3"]
    if kind == "prefix_last":
        return ["k0", "k1", "v0", "v1", "v2", "v3", "p0", "p1"]
    seq = ["q0", "k0", "q1", "k1", "v0", "v1", "v2", "v3", "r0", "r1", "r2", "r3", "p0", "p1"]
    seq += ["g%d" % i for i in range(8)]
    for i in range(4):
        seq += ["wa%d" % i, "wb%d" % i]
    seq += ["wo%d" % i for i in range(4)]
    for i in range(11):
        seq += ["fg%d" % i, "fu%d" % i]
    for n in range(4):
        seq += ["fd%d_%d" % (n, kp) for kp in range(3)]
    return seq


def build_program(n_prefix=NT, n_main=NT):
    nc = bass.Bass("TRN2", target_bir_lowering=False)
    dram = {}

    def din(name, shape, dt=F32):
        dram[name] = nc.dram_tensor(name, list(shape), dt, kind="ExternalInput").ap()
        return dram[name]

    x_main = din("x_main", [TOK, D])
    x_prev = din("x_prev", [TOK, D])
    din("w_in", [D, DIN])
    wgu_d = din("wgu_aug", [32, 1024])
    wpool_d = din("w_pool", [4, 256, 256])
    din("w_branch_a", [1024, D])
    din("w_branch_b", [D, D])
    din("w_out", [D, D])
    din("w_ffn_gate", [D, DFF])
    din("w_ffn_up", [D, DFF])
    din("w_ffn_down", [DFF, D])
    colv_d = din("colv", [128, NCOLV])
    gbc_d = din("gbc", [2, 128, D])
    ident_d = din("ident", [128, 128], BF16)
    maskT_d = din("maskT", [128, 128], BF16)
    tri_d = din("tri", [128, 128])
    invc_d = din("invc", [2, 128, 8, 16])
    out_d = nc.dram_tensor("out", [TOK, D], F32, kind="ExternalOutput").ap()
    cat = _panel_catalog()
    pidx = {c[0]: i for i, c in enumerate(cat)}
    wscr = nc.dram_tensor("wscr", [len(cat), 128, 16 * 512], BF16, kind="Internal").ap()

    es = ExitStack()
    with es:
        def sbuf(name, nfloats):
            return es.enter_context(nc.sbuf_tensor(name, [128, nfloats], F32))

        def sem(name):
            return es.enter_context(nc.semaphore(name))

        S = Sched()
        for n_, pe_ in (("pe", True), ("act", False), ("dve", False), ("pool", False), ("sp", False)):
            S.add_engine(n_, sem("s_" + n_), is_pe=pe_)

        def region(name, nbytes):
            return (name, sbuf(name, nbytes // 4)[:])

        KB = 1024
        rP = region("rP", 34 * KB)
        rX = region("rX", 16 * KB)
        rH = region("rH", 16 * KB)
        rA = region("rA", 28 * KB)
        rB = region("rB", 24 * KB)
        rC = region("rC", 10 * KB)
        rW = region("rW", NSLOT * 16 * KB)
        rT = region("rT", 11 * KB)

        def mk(reg, off, shape, dt):
            return Buf(reg[0], reg[1], off, shape, dt)

        Shat = mk(rP, 0, [8, 512], F32)
        Sbf = mk(rP, 16 * KB, [8, 512], BF16)
        pbuf = mk(rP, 24 * KB, [8, 16 + TT], F32)
        gbc = mk(rA, 16 * KB, [2048], F32)
        Dall = mk(rP, 24 * KB + 8704, [8, 4], F32)
        xres = mk(rX, 0, [NJ, D], F32)
        hT = mk(rH, 0, [16, TT], BF16)
        oT = mk(rH, 8 * KB, [16, TT], BF16)
        la = mk(rA, 0, [NJ, 1024], F32)
        qT = mk(rA, 8 * KB, [8, TT], BF16)
        kT = mk(rA, 12 * KB, [8, TT], BF16)
        ktm = mk(rA, 16 * KB, [NJ, 1024], BF16)
        vtm = mk(rA, 20 * KB, [NJ, D], BF16)
        gates = mk(rA, 0, [32, TT], BF16)
        mtm = mk(rA, 0, [NJ, D], F32)
        xs = mk(rB, 0, [NJ, D], BF16)
        rs = mk(rB, 8 * KB, [16, TT], BF16)
        ptmp0 = mk(rB, 0, [2, 16 + TT], F32)
        ptmp1 = mk(rB, 2176, [2, 16 + TT], F32)
        dT = mk(rB, 4352 + 256, [8, TT], BF16)
        ypT = mk(rB, 4608 + 4 * KB, [8, TT], BF16)
        mixT = mk(rB, 16 * KB, [16, TT], BF16)
        aT = mk(rB, 0, [44, TT], BF16)
        ident = mk(rC, 0, [128], BF16)
        maskT = mk(rC, 256, [128], BF16)
        tri = mk(rC, 512, [128], F32)
        colv = mk(rC, 1024, [NCOLV], F32)
        wg = mk(rC, 1536, [16, 16], BF16)
        wgu = mk(rC, 2048, [1024], BF16)
        wpool = mk(rC, 4096, [4, 2, 256], BF16)
        glr = mk(rC, 8192, [TT], BF16)
        invc = mk(rC, 8704, [2, 8, 16], F32)
        stat = mk(rC, 9728, [64], F32)
        junk = mk(rB, 16 * KB, [2048], BF16)
        eq = [mk(rT, 0 * KB + i * KB, [TT], F32) for i in range(2)]
        ek = [mk(rT, 2 * KB + i * KB, [TT], F32) for i in range(2)]
        etmp = [mk(rT, 4 * KB + i * 2 * KB, [512], F32) for i in range(2)]
        atm = [mk(rT, 8 * KB + i * 256, [128], BF16) for i in range(2)]
        osb = [mk(rT, 8 * KB + 512 + i * KB, [512], BF16) for i in range(2)]
        slots = [mk(rW, i * 16 * KB, [16, 512], BF16) for i in range(NSLOT)]

        psf = [es.enter_context(nc.psum_tensor("psf%d" % i, [128, 512], F32)) for i in range(6)]
        psb = [es.enter_context(nc.psum_tensor("psb%d" % i, [128, 1024], BF16)) for i in range(2)]
        bank_ctr = [0, 0]

        def nbf():
            b = bank_ctr[0] % 6
            bank_ctr[0] += 1
            return psf[b], [("psf", b)]

        def nbb():
            b = bank_ctr[1] % 2
            bank_ctr[1] += 1
            return psb[b], [("psb", b)]

        s_const = sem("s_const")
        s_slot = [sem("s_slot%d" % i) for i in range(NSLOT)]
        s_x = sem("s_x")
        s_out = sem("s_out")
        s_gbc = sem("s_gbc")

        consts = [
            (ident, ident_d, "sp"), (maskT, maskT_d, "sp"), (tri, tri_d, "sp"),
            (colv, colv_d, "sp"),
        ]
        nconst = len(consts) + 1
        for i, (b, src, q) in enumerate(consts):
            S.dma("sp", lambda e, b=b, src=src: e.dma_start(out=b.ap, in_=src), s_const,
                  writes=b.keys(), group=nconst, idx=i)
        S.dma("sp", lambda e: e.dma_start(out=invc.ap, in_=invc_d.rearrange("a p c t -> p a c t")),
              s_const, writes=invc.keys(), group=nconst, idx=nconst - 1)
        s_small = sem("s_small")
        w_in_d = dram["w_in"]
        st_wg = mk(rT, 4 * KB, [16, 16], F32)
        st_wgu = mk(rB, 16 * KB, [1024], F32)
        st_wp = mk(rA, 0, [4, 2, 256], F32)
        wg_src = w_in_d[:, OFF_G:OFF_G + 16].rearrange("(kc p) c -> p kc c", p=128)
        for i in range(8):
            S.dma("sp", lambda e, i=i: e.dma_start(out=st_wg.ap[:, 2 * i:2 * i + 2, :], in_=wg_src[:, 2 * i:2 * i + 2, :]),
                  s_small, writes=st_wg.ck(2 * i, 2), group=13, idx=i)
        S.dma("sp", lambda e: e.dma_start(out=st_wgu.ap[0:32, :], in_=wgu_d), s_small, writes=st_wgu.keys(), group=13, idx=8)
        wp_src = wpool_d.rearrange("g (kc p) d -> p g kc d", p=128)
        for g in range(4):
            S.dma("sp", lambda e, g=g: e.dma_start(out=st_wp.ap[:, g, :, :], in_=wp_src[:, g, :, :]),
                  s_small, writes=st_wp.ck(g), group=13, idx=9 + g)
        S.op("dve", lambda e: e.tensor_copy(out=wg.ap, in_=st_wg.ap), reads=st_wg.keys(), writes=wg.keys())
        S.op("dve", lambda e: e.tensor_copy(out=wgu.ap[0:32, :], in_=st_wgu.ap[0:32, :]), reads=st_wgu.keys(), writes=wgu.keys())
        S.op("dve", lambda e: e.tensor_copy(out=wpool.ap, in_=st_wp.ap), reads=st_wp.keys(), writes=wpool.keys())
        catd = {c[0]: c for c in cat}
        stf = [mk(rW, 0, [16, 512], F32), mk(rW, 32 * KB, [16, 512], F32)]
        stb = [mk(rB, 0, [16, 512], BF16), mk(rH, 0, [16, 512], BF16)]
        s_ld = [sem("s_ld0"), sem("s_ld1")]
        s_st = [sem("s_st0"), sem("s_st1")]
        early = ["k0", "k1", "v0", "v1", "v2", "v3", "p0", "p1"] if n_prefix else [c[0] for c in cat]
        for n, name in enumerate(early):
            _, src, k0c, nkc, c0, ncols = catd[name]
            bb = n % 2
            sap = dram[src][k0c * 128:(k0c + nkc) * 128, c0:c0 + 512].rearrange("(kc p) c -> p kc c", p=128)
            npc = nkc // 2
            for i in range(npc):
                S.dma("sp", lambda e, i=i, bb=bb, sap=sap: e.dma_start(
                    out=stf[bb].ap[:, 2 * i:2 * i + 2, :], in_=sap[:, 2 * i:2 * i + 2, :]),
                    s_ld[bb], writes=stf[bb].ck(2 * i, 2), group=npc, idx=i)
            S.op("dve", lambda e, bb=bb, nkc=nkc: e.tensor_copy(out=stb[bb].ap[:, 0:nkc, :], in_=stf[bb].ap[:, 0:nkc, :]),
                 reads=stf[bb].keys(), writes=stb[bb].keys())
            dap = wscr[pidx[name]].rearrange("p (kc c) -> p kc c", c=512)[:, 0:nkc, :]
            S.dma("act", lambda e, bb=bb, nkc=nkc, dap=dap: e.dma_start(out=dap, in_=stb[bb].ap[:, 0:nkc, :]),
                  s_st[bb], reads=stb[bb].keys(), writes=[("wscr", name, q0) for q0 in range(0, nkc, 4)])
        stfS = mk(rB, 8 * KB, [4, 512], F32)
        stbS = [mk(rH, 8 * KB, [4, 512], BF16), mk(rH, 12 * KB, [4, 512], BF16)]
        s_ldS = sem("s_ldS")
        s_stS = [sem("s_stS0"), sem("s_stS1")]
        conv_pending = []
        seen_p = set(early)
        for name in _tile_panels("main"):
            if name in seen_p:
                continue
            seen_p.add(name)
            for q0 in range(0, catd[name][3], 4):
                conv_pending.append((name, q0))
        qctr = [0]

        def conv_some(k):
            for _ in range(k):
                if not conv_pending:
                    return
                name, q0 = conv_pending.pop(0)
                _, src, k0c, nkc, c0, ncols = catd[name]
                sap = dram[src][(k0c + q0) * 128:(k0c + q0 + 4) * 128, c0:c0 + 512].rearrange("(kc p) c -> p kc c", p=128)
                for i in range(2):
                    S.dma("sp", lambda e, i=i, sap=sap: e.dma_start(
                        out=stfS.ap[:, 2 * i:2 * i + 2, :], in_=sap[:, 2 * i:2 * i + 2, :]),
                        s_ldS, writes=stfS.ck(2 * i, 2), group=2, idx=i)
                bb = qctr[0] % 2
                qctr[0] += 1
                S.op("dve", lambda e, bb=bb: e.tensor_copy(out=stbS[bb].ap, in_=stfS.ap),
                     reads=stfS.keys(), writes=stbS[bb].keys())
                dap = wscr[pidx[name]].rearrange("p (kc c) -> p kc c", c=512)[:, q0:q0 + 4, :]
                S.dma("act", lambda e, bb=bb, dap=dap: e.dma_start(out=dap, in_=stbS[bb].ap),
                      s_stS[bb], reads=stbS[bb].keys(), writes=[("wscr", name, q0)])

        conv_rate = [max(3, -(-len(conv_pending) // max(1, 6 * n_prefix - 4)))]

        def conv_force(name):
            while any(nm == name for nm, _ in conv_pending):
                conv_some(1)
        S.op("dve", lambda e: e.memset(Shat.ap, 0.0), writes=Shat.keys())
        S.op("dve", lambda e: e.memset(Sbf.ap, 0.0), writes=Sbf.keys())
        S.op("dve", lambda e: e.memset(pbuf.ap, 0.0), writes=pbuf.keys())
        S.op("dve", lambda e: e.memset(Dall.ap, 1.0), writes=Dall.keys())
        S.op("dve", lambda e: e.memset(glr.ap[0:32, :], 1.0), writes=glr.keys())

        kinds = ["prefix"] * n_prefix
        if n_prefix:
            kinds[-1] = "prefix_last"
        kinds += ["main"] * n_main
        plist = []
        for kd in kinds:
            plist += _tile_panels(kd)
        pstate = {"cons": 0, "nload": 0}

        def issue_load(n):
            name = plist[n]
            conv_force(name)
            _, src, k0c, nkc, c0, ncols = catd[name]
            s = n % NSLOT
            sl = slots[s]
            S.dma("sp", lambda e, sl=sl, name=name, nkc=nkc: e.dma_start(
                out=sl.ap[:, 0:nkc, :],
                in_=wscr[pidx[name]].rearrange("p (kc c) -> p kc c", c=512)[:, 0:nkc, :]),
                s_slot[s], reads=[("wscr", name, q0) for q0 in range(0, nkc, 4)], writes=sl.keys())

        def acquire(name):
            assert plist[pstate["cons"]] == name, (plist[pstate["cons"]], name)
            while pstate["nload"] < len(plist) and pstate["nload"] < pstate["cons"] + NSLOT - 1:
                issue_load(pstate["nload"])
                pstate["nload"] += 1
            conv_some(conv_rate[0])
            s = pstate["cons"] % NSLOT
            pstate["cons"] += 1
            return slots[s]

        flip = {"ev": 0}

        _EV = os.environ.get("EVAC", "")

        def evac_engine():
            flip["ev"] ^= 1
            if _EV:
                return _EV
            return "dve" if flip["ev"] else "act"

        def copy_op(eng, out_ap, in_ap, reads, writes):
            if eng == "act":
                S.op("act", lambda e: e.activation(out=out_ap, in_=in_ap, func=AF.Copy), reads=reads, writes=writes)
            else:
                S.op(eng, lambda e: e.tensor_copy(out=out_ap, in_=in_ap), reads=reads, writes=writes)

        def scale_op(eng, out_ap, in_ap, scal_ap, reads, writes):
            if eng == "act":
                S.op("act", lambda e: e.activation(out=out_ap, in_=in_ap, func=AF.Copy, scale=scal_ap),
                     reads=reads, writes=writes)
            else:
                S.op(eng, lambda e: e.tensor_scalar_mul(out=out_ap, in0=in_ap, scalar1=scal_ap),
                     reads=reads, writes=writes)

        def mm_group(out_ap, pairs, reads_list, wkeys):
            n = len(pairs)
            for i, (l, r) in enumerate(pairs):
                S.op("pe", lambda e, l=l, r=r, i=i: e.matmul(out_ap, lhsT=l, rhs=r, start=(i == 0), stop=(i == n - 1)),
                     reads=reads_list[i] if isinstance(reads_list[0], list) else reads_list,
                     writes=wkeys, inc=(i == n - 1))

        cv = colv.ap
        st = stat.ap
        stat_k = stat.keys()

        def rstd_from_ms(ms_ap, tmp_ap, out_ap):
            S.op("act", lambda e: e.activation(out=tmp_ap, in_=ms_ap, func=AF.Ln, bias=EPS), reads=stat_k, writes=stat_k)
            S.op("act", lambda e: e.activation(out=out_ap, in_=tmp_ap, func=AF.Exp, scale=-0.5), reads=stat_k, writes=stat_k)

        def stage_norm_T(gcol):
            for j in range(NJ):
                S.op("act", lambda e, j=j: e.activation(out=junk.ap, in_=xres.ap[:, j, :], func=AF.Square,
                                                         scale=RS_D, accum_out=st[:, j:j + 1]),
                     reads=xres.ck(j), writes=junk.keys() + stat_k)
            if _STOPB <= 1:
                return
            rstd_from_ms(st[:, 0:NJ], st[:, 2:2 + NJ], st[:, 4:4 + NJ])
            if _STOPB <= 2:
                return
            for j in range(NJ):
                S.op("dve", lambda e, j=j: e.tensor_scalar_mul(out=xs.ap[:, j, :], in0=xres.ap[:, j, :],
                                                                scalar1=st[:, 4 + j:5 + j]),
                     reads=xres.ck(j) + stat_k, writes=xs.ck(j))
            if _STOPB <= 3:
                return
            for k4 in range(4):
                if _STOPB <= 4 and k4 >= 1:
                    return
                pb, pk = nbb()
                pv = pb[:].rearrange("p (a t) -> p a t", t=TT)
                for kk in range(4):
                    kc = k4 * 4 + kk
                    for j in range(NJ):
                        S.op("pe", lambda e, kk=kk, kc=kc, j=j, pv=pv: e.transpose(
                            out=pv[:, kk, j * 128:(j + 1) * 128], in_=xs.ap[:, j, kc * 128:(kc + 1) * 128],
                            identity=ident.ap),
                            reads=xs.ck(j) + ident.keys(), writes=pk, inc=(kk == 3 and j == NJ - 1))
                ev_eng = evac_engine()
                for kk in range(4):
                    kc = k4 * 4 + kk
                    scale_op(ev_eng, hT.ap[:, kc, :], pv[:, kk, :], cv[:, gcol + kc:gcol + kc + 1],
                             reads=pk + colv.keys(), writes=hT.ck(kc))

        def proj_fm(slot, cc, nkc, rhs_buf, ncol=TT, col0=0):
            pb, pk = nbf()
            out_ap = pb[:, 0:ncol]
            pairs = [(slot.ap[:, kc, cc * 128:(cc + 1) * 128], rhs_buf.ap[:, kc, col0:col0 + ncol]) for kc in range(nkc)]
            reads = [slot.ck(kc) + rhs_buf.ck(kc) for kc in range(nkc)]
            mm_group(out_ap, pairs, reads, pk)
            return out_ap, pk

        gchunk = {"g": 0}

        def load_x(src, t):
            xv = src[t * TT:(t + 1) * TT, :].rearrange("(j p) d -> p j d", p=128)
            S.dma("act", lambda e: e.dma_start(out=xres.ap, in_=xv), s_x, writes=xres.keys())

        tile_seq = []
        xstate = {"i": 0, "loaded": False}

        def tile(kind, t):
            main = kind == "main"
            src = x_main if main else x_prev
            if not xstate["loaded"]:
                load_x(src, t)
            xstate["loaded"] = False
            stage_norm_T(C_GPRE)
            xstate["i"] += 1
            if not main and _STOP > 1 and xstate["i"] < len(tile_seq):
                nk, nt_ = tile_seq[xstate["i"]]
                load_x(x_main if nk == "main" else x_prev, nt_)
                xstate["loaded"] = True
            if _STOP <= 1:
                pstate['cons'] = len(plist); return
            g0 = gchunk["g"]
            gchunk["g"] += NJ
            pb, pk = nbf()
            mm_group(pb[0:16, 0:TT],
                     [(wg.ap[:, kc, :], hT.ap[:, kc, :]) for kc in range(16)],
                     [wg.keys() + hT.ck(kc) for kc in range(16)], pk)
            copy_op("dve", glr.ap[0:16, :], pb[0:16, 0:TT], pk, glr.keys())
            for j in range(NJ):
                for half in range(2):
                    pb, pk = nbf()
                    mm_group(pb[:, 0:512], [(glr.ap[0:32, j * 128:(j + 1) * 128], wgu.ap[0:32, half * 512:(half + 1) * 512])],
                             glr.keys() + wgu.keys(), pk)
                    et = etmp[(j * 2 + half) % 2]
                    S.op("act", lambda e, pb=pb, et=et: e.activation(out=et.ap, in_=pb[:, 0:512], func=AF.Exp, scale=-1.0),
                         reads=pk, writes=et.keys())
                    S.op("act", lambda e, et=et, j=j, half=half: e.activation(
                        out=la.ap[:, j, half * 512:(half + 1) * 512], in_=et.ap, func=AF.Ln, bias=1.0),
                        reads=et.keys(), writes=la.keys(j * 4096 + half * 2048, j * 4096 + (half + 1) * 2048))
            if _STOP <= 2:
                pstate['cons'] = len(plist); return
            def kT_transposes(c):
                pb2, pbk = nbb()
                for j in range(NJ):
                    S.op("pe", lambda e, pb2=pb2, j=j, c=c: e.transpose(
                        out=pb2[:, j * 128:(j + 1) * 128], in_=kT.ap[:, c, j * 128:(j + 1) * 128], identity=ident.ap),
                        reads=kT.ck(c) + ident.keys(), writes=pbk, inc=(j == NJ - 1))
                copy_op("act", ktm.ap[:, :, c * 128:(c + 1) * 128],
                        pb2[:, 0:NJ * 128].rearrange("p (j f) -> p j f", f=128), pbk, ktm.keys())

            slot_q = slot_k = None
            for c in range(8):
                if c % 4 == 0:
                    if main:
                        slot_q = acquire("q%d" % (c // 4))
                    slot_k = acquire("k%d" % (c // 4))
                cc = c % 4
                pg, pgk = nbf()
                for j in range(NJ):
                    S.op("pe", lambda e, pg=pg, j=j, c=c: e.matmul(
                        pg[:, j * 128:(j + 1) * 128], lhsT=la.ap[:, j, c * 128:(c + 1) * 128], rhs=tri.ap,
                        start=True, stop=True),
                        reads=la.keys(j * 4096 + c * 512, j * 4096 + (c + 1) * 512) + tri.keys(), writes=pgk,
                        inc=(j == NJ - 1))
                eqb, ekb = eq[c % 2], ek[c % 2]
                if main:
                    S.op("act", lambda e, pg=pg, eqb=eqb: e.activation(out=eqb.ap, in_=pg[:, 0:TT], func=AF.Exp,
                                                                      bias=LN_QSCALE), reads=pgk, writes=eqb.keys())
                S.op("act", lambda e, pg=pg, ekb=ekb: e.activation(out=ekb.ap, in_=pg[:, 0:TT], func=AF.Exp, scale=-1.0),
                     reads=pgk, writes=ekb.keys())
                for j in range(NJ):
                    dcol = (g0 + j) % 4
                    S.op("act", lambda e, pg=pg, j=j, c=c, dcol=dcol: e.activation(
                        out=Dall.ap[:, c, dcol:dcol + 1], in_=pg[:, j * 128 + 127:j * 128 + 128], func=AF.Exp),
                        reads=pgk, writes=Dall.keys())
                if main:
                    po, pok = proj_fm(slot_q, cc, 16, hT)
                    S.op("dve", lambda e, po=po, eqb=eqb, c=c: e.tensor_tensor(out=qT.ap[:, c, :], in0=po, in1=eqb.ap,
                                                                            op=ALU.mult),
                         reads=pok + eqb.keys(), writes=qT.ck(c))
                po, pok = proj_fm(slot_k, cc, 16, hT)
                S.op("dve", lambda e, po=po, ekb=ekb, c=c: e.tensor_tensor(out=kT.ap[:, c, :], in0=po, in1=ekb.ap,
                                                                        op=ALU.mult),
                     reads=pok + ekb.keys(), writes=kT.ck(c))
                if c >= 1:
                    kT_transposes(c - 1)
            kT_transposes(7)
            if _STOP <= 3:
                pstate['cons'] = len(plist); return
            for n in range(4):
                sl = acquire("v%d" % n)
                for j in range(NJ):
                    pb, pk = nbf()
                    mm_group(pb[:, 0:512],
                             [(hT.ap[:, kc, j * 128:(j + 1) * 128], sl.ap[:, kc, :]) for kc in range(16)],
                             [hT.ck(kc) + sl.ck(kc) for kc in range(16)], pk)
                    copy_op(evac_engine(), vtm.ap[:, j, n * 512:(n + 1) * 512], pb[:, 0:512], pk,
                            vtm.keys(j * 4096 + n * 1024, j * 4096 + (n + 1) * 1024))
            if _STOP <= 4:
                pstate['cons'] = len(plist); return
            if main:
                for n in range(4):
                    sl = acquire("r%d" % n)
                    for cc in range(4):
                        fc = n * 4 + cc
                        po, pok = proj_fm(sl, cc, 16, hT)
                        et = etmp[fc % 2]
                        S.op("act", lambda e, po=po, et=et: e.activation(out=et.ap[:, 0:TT], in_=po, func=AF.Silu),
                             reads=pok, writes=et.keys())
                        S.op("dve", lambda e, et=et, fc=fc: e.tensor_scalar_mul(
                            out=rs.ap[:, fc, :], in0=et.ap[:, 0:TT], scalar1=cv[:, C_GN + fc % 4:C_GN + fc % 4 + 1]),
                            reads=et.keys() + colv.keys(), writes=rs.ck(fc))
            items = [(j, h) for j in range(NJ) for h in range(4)]
            ctx = {}

            def fixed_bank(b):
                return psf[b], [("psf", b)]

            def part1(i):
                j, h = items[i]
                js = slice(j * 128, (j + 1) * 128)
                pa, pak = fixed_bank(i % 2)
                mm_group(pa[:, 0:128],
                         [(kT.ap[:, 2 * h + kk, js], qT.ap[:, 2 * h + kk, js]) for kk in range(2)],
                         [kT.ck(2 * h + kk) + qT.ck(2 * h + kk) for kk in range(2)], pak)
                am = atm[i % 2]
                S.op("dve", lambda e, pa=pa, am=am: e.tensor_tensor(out=am.ap, in0=pa[:, 0:128], in1=maskT.ap,
                                                                 op=ALU.mult),
                     reads=pak + maskT.keys(), writes=am.keys())

            def part2(i):
                j, h = items[i]
                gi = g0 + j
                dcur, dprev = gi % 4, (gi - 1) % 4
                js = slice(j * 128, (j + 1) * 128)
                vslice = vtm.ap[:, j, h * 512:(h + 1) * 512]
                vkeys = vtm.keys(j * 4096 + h * 1024, j * 4096 + (h + 1) * 1024)
                if main:
                    am = atm[i % 2]
                    po, pok = fixed_bank(2 + i % 2)
                    mm_group(po[:, 0:512],
                             [(qT.ap[:, 2 * h, js], Sbf.ap[:, 2 * h, :]),
                              (qT.ap[:, 2 * h + 1, js], Sbf.ap[:, 2 * h + 1, :]),
                              (am.ap, vslice)],
                             [qT.ck(2 * h) + Sbf.ck(2 * h), qT.ck(2 * h + 1) + Sbf.ck(2 * h + 1), am.keys() + vkeys],
                             pok)
                for kk in range(2):
                    fc = 2 * h + kk
                    ps_, psk = fixed_bank(4 + kk)
                    mm_group(ps_[:, 0:512], [(ktm.ap[:, j, fc * 128:(fc + 1) * 128], vslice)],
                             ktm.keys(j * 2048 + fc * 256, j * 2048 + (fc + 1) * 256) + vkeys, psk)
                    S.op("dve", lambda e, ps_=ps_, fc=fc, dprev=dprev: e.scalar_tensor_tensor(
                        out=Shat.ap[:, fc, :], in0=Shat.ap[:, fc, :], scalar=Dall.ap[:, fc, dprev:dprev + 1],
                        in1=ps_[:, 0:512], op0=ALU.mult, op1=ALU.add),
                        reads=psk + Shat.ck(fc) + Dall.keys(), writes=Shat.ck(fc))
                    if not main:
                        S.op("act", lambda e, fc=fc, dcur=dcur: e.activation(
                            out=Sbf.ap[:, fc, :], in_=Shat.ap[:, fc, :], func=AF.Copy, scale=Dall.ap[:, fc, dcur:dcur + 1]),
                            reads=Shat.ck(fc) + Dall.keys(), writes=Sbf.ck(fc))
                if main:
                    si = 8 + i
                    S.op("act", lambda e, po=po, si=si: e.activation(out=junk.ap[:, 0:512], in_=po[:, 0:512],
                                                                  func=AF.Square, scale=RS_V,
                                                                  accum_out=st[:, si:si + 1]),
                         reads=pok, writes=junk.keys() + stat_k)
                    rstd_from_ms(st[:, si:si + 1], st[:, si + 16:si + 17], st[:, si + 32:si + 33])
                    ob = osb[i % 2]
                    S.op("dve", lambda e, po=po, ob=ob, si=si: e.tensor_scalar_mul(
                        out=ob.ap, in0=po[:, 0:512], scalar1=st[:, si + 32:si + 33]),
                        reads=pok + stat_k, writes=ob.keys())
                    for kk in range(2):
                        fc = 2 * h + kk
                        S.op("act", lambda e, fc=fc, dcur=dcur: e.activation(
                            out=Sbf.ap[:, fc, :], in_=Shat.ap[:, fc, :], func=AF.Copy, scale=Dall.ap[:, fc, dcur:dcur + 1]),
                            reads=Shat.ck(fc) + Dall.keys(), writes=Sbf.ck(fc))

            def part3(i):
                j, h = items[i]
                js = slice(j * 128, (j + 1) * 128)
                ob = osb[i % 2]
                pb2, pbk = nbb()
                for qd in range(4):
                    S.op("pe", lambda e, pb2=pb2, qd=qd, ob=ob: e.transpose(
                        out=pb2[:, qd * 128:(qd + 1) * 128], in_=ob.ap[:, qd * 128:(qd + 1) * 128],
                        identity=ident.ap),
                        reads=ob.keys() + ident.keys(), writes=pbk, inc=(qd == 3))
                S.op("dve", lambda e, pb2=pb2, h=h, js=js: e.tensor_tensor(
                    out=oT.ap[:, 4 * h:4 * h + 4, js], in0=pb2[:, 0:512].rearrange("p (a t) -> p a t", t=128),
                    in1=rs.ap[:, 4 * h:4 * h + 4, js], op=ALU.mult),
                    reads=pbk + rs.ck(4 * h, 4), writes=oT.ck(4 * h, 4))

            if main:
                part1(0)
            for i in range(len(items)):
                if main and i + 1 < len(items):
                    part1(i + 1)
                part2(i)
                if main and i >= 1:
                    part3(i - 1)
            if main:
                part3(len(items) - 1)
            if _STOP <= 5:
                pstate['cons'] = len(plist); return
            if kind == "prefix":
                return
            for n in range(2):
                sl = acquire("p%d" % n)
                for cc in range(4):
                    fc = n * 4 + cc
                    if main:
                        po, pok = proj_fm(sl, cc, 16, hT)
                        copy_op(evac_engine(), pbuf.ap[:, fc, 16:16 + TT], po, pok, pbuf.ck(fc))
                    else:
                        po, pok = proj_fm(sl, cc, 16, hT, ncol=16, col0=TT - 16)
                        copy_op(evac_engine(), pbuf.ap[:, fc, 0:16], po, pok, pbuf.ck(fc))
            if not main:
                return
            first = (t == 0)
            W = 16 + TT
            for g in range(4):
                w = 2 << g
                srcp = pbuf.ap[:, 2 * g:2 * g + 2, :]
                skeys = pbuf.ck(2 * g, 2)
                cur, curk = srcp, skeys
                bufs = [ptmp0, ptmp1]
                sh = 1
                bi = 0
                while sh < w:
                    ob_ = bufs[bi]
                    S.op("pool", lambda e, ob_=ob_, cur=cur, sh=sh: e.tensor_tensor(
                        out=ob_.ap[:, :, 2 * sh - 1:W], in0=cur[:, :, 2 * sh - 1:W], in1=cur[:, :, sh - 1:W - sh],
                        op=ALU.add), reads=curk, writes=ob_.keys())
                    cur, curk = ob_.ap, ob_.keys()
                    sh *= 2
                    bi ^= 1
                S.op("dve", lambda e, cur=cur, srcp=srcp, g=g, w=w: e.scalar_tensor_tensor(
                    out=dT.ap[:, 2 * g:2 * g + 2, :], in0=cur[:, :, 16:W], scalar=1.0 / w, in1=srcp[:, :, 16:W],
                    op0=ALU.mult, op1=ALU.subtract), reads=curk + skeys, writes=dT.ck(2 * g, 2))
                iv = invc.ap[:, 0 if first else 1, 2 * g:2 * g + 2, :]
                other = bufs[bi]
                S.op("pool", lambda e, cur=cur, iv=iv, other=other: e.tensor_tensor(
                    out=other.ap[:, :, 0:16], in0=cur[:, :, 16:32], in1=iv, op=ALU.mult),
                    reads=curk + invc.keys(), writes=other.keys())
                S.op("pool", lambda e, other=other, srcp=srcp, g=g: e.tensor_tensor(
                    out=dT.ap[:, 2 * g:2 * g + 2, 0:16], in0=other.ap[:, :, 0:16], in1=srcp[:, :, 16:32],
                    op=ALU.subtract), reads=other.keys() + skeys, writes=dT.ck(2 * g, 2))
            S.op("pool", lambda e: e.tensor_copy(out=pbuf.ap[:, :, 0:16], in_=pbuf.ap[:, :, TT:TT + 16]),
                 reads=pbuf.keys(), writes=pbuf.keys())
            for n in range(8):
                sl = acquire("g%d" % n)
                for cc in range(4):
                    fc = n * 4 + cc
                    po, pok = proj_fm(sl, cc, 16, hT)
                    S.op("act", lambda e, po=po, fc=fc: e.activation(
                        out=gates.ap[:, fc, :], in_=po, func=AF.Sigmoid, bias=cv[:, C_BG + fc:C_BG + fc + 1]),
                        reads=pok + colv.keys(), writes=gates.ck(fc))
            for g in range(4):
                for m in range(2):
                    pb, pk = nbf()
                    mm_group(pb[:, 0:TT],
                             [(wpool.ap[:, g, kc, m * 128:(m + 1) * 128], dT.ap[:, 2 * g + kc, :]) for kc in range(2)],
                             [wpool.keys() + dT.ck(2 * g + kc) for kc in range(2)], pk)
                    fc = 2 * g + m
                    scale_op(evac_engine(), ypT.ap[:, fc, :], pb[:, 0:TT], cv[:, C_PSC + fc:C_PSC + fc + 1],
                             reads=pk + colv.keys(), writes=ypT.ck(fc))
            for n in range(4):
                sla = acquire("wa%d" % n)
                slb = acquire("wb%d" % n)
                for cc in range(4):
                    m = n * 4 + cc
                    pa_, pak = proj_fm(sla, cc, 8, ypT)
                    ta = etmp[0]
                    S.op("dve", lambda e, pa_=pa_, ta=ta, m=m: e.tensor_tensor(
                        out=ta.ap[:, 0:TT], in0=pa_, in1=gates.ap[:, m, :], op=ALU.mult),
                        reads=pak + gates.ck(m), writes=ta.keys())
                    pb_, pbk_ = proj_fm(slb, cc, 16, oT)
                    tb = etmp[1]
                    S.op("dve", lambda e, pb_=pb_, tb=tb, m=m: e.tensor_tensor(
                        out=tb.ap[:, 0:TT], in0=pb_, in1=gates.ap[:, 16 + m, :], op=ALU.mult),
                        reads=pbk_ + gates.ck(16 + m), writes=tb.keys())
                    S.op("dve", lambda e, ta=ta, tb=tb, m=m: e.tensor_tensor(
                        out=mixT.ap[:, m, :], in0=ta.ap[:, 0:TT], in1=tb.ap[:, 0:TT], op=ALU.add),
                        reads=ta.keys() + tb.keys(), writes=mixT.ck(m))
            S.dma("act", lambda e: e.dma_start(out=gbc.ap, in_=gbc_d[0]), s_gbc, writes=gbc.keys())
            for n in range(4):
                sl = acquire("wo%d" % n)
                for j in range(NJ):
                    pb, pk = nbf()
                    mm_group(pb[:, 0:512],
                             [(mixT.ap[:, kc, j * 128:(j + 1) * 128], sl.ap[:, kc, :]) for kc in range(16)],
                             [mixT.ck(kc) + sl.ck(kc) for kc in range(16)], pk)
                    copy_op(evac_engine(), mtm.ap[:, j, n * 512:(n + 1) * 512], pb[:, 0:512], pk,
                            mtm.keys(j * 8192 + n * 2048, j * 8192 + (n + 1) * 2048))
            post_norm_residual()
            stage_norm_T(C_GFFN)
            for i in range(11):
                slg = acquire("fg%d" % i)
                slu = acquire("fu%d" % i)
                for cc in range(4):
                    fc = i * 4 + cc
                    pg_, pgk_ = proj_fm(slg, cc, 16, hT)
                    pu_, puk_ = proj_fm(slu, cc, 16, hT)
                    et = etmp[fc % 2]
                    S.op("act", lambda e, pg_=pg_, et=et: e.activation(out=et.ap[:, 0:TT], in_=pg_, func=AF.Silu),
                         reads=pgk_, writes=et.keys())
                    S.op("dve", lambda e, pu_=pu_, et=et, fc=fc: e.tensor_tensor(
                        out=aT.ap[:, fc, :], in0=pu_, in1=et.ap[:, 0:TT], op=ALU.mult),
                        reads=puk_ + et.keys(), writes=aT.ck(fc))
            S.dma("act", lambda e: e.dma_start(out=gbc.ap, in_=gbc_d[1]), s_gbc, writes=gbc.keys())
            for n in range(4):
                banks = [nbf() for _ in range(NJ)]
                for kp in range(3):
                    sl = acquire("fd%d_%d" % (n, kp))
                    nkc = 16 if kp < 2 else 12
                    for j in range(NJ):
                        pb, pk = banks[j]
                        for kc in range(nkc):
                            f = kp * 16 + kc
                            S.op("pe", lambda e, pb=pb, f=f, j=j, kc=kc, sl=sl, kp=kp, nkc=nkc: e.matmul(
                                pb[:, 0:512], lhsT=aT.ap[:, f, j * 128:(j + 1) * 128], rhs=sl.ap[:, kc, :],
                                start=(f == 0), stop=(f == 43)),
                                reads=aT.ck(f) + sl.ck(kc), writes=pk, inc=(kc == nkc - 1))
                for j in range(NJ):
                    pb, pk = banks[j]
                    copy_op(evac_engine(), mtm.ap[:, j, n * 512:(n + 1) * 512], pb[:, 0:512], pk,
                            mtm.keys(j * 8192 + n * 2048, j * 8192 + (n + 1) * 2048))
            post_norm_residual(final=True)
            ov = out_d[t * TT:(t + 1) * TT, :].rearrange("(j p) d -> p j d", p=128)
            S.dma("act", lambda e: e.dma_start(out=ov, in_=mtm.ap), s_out, reads=mtm.keys(), writes=[("out", t)])

        def post_norm_residual(final=False):
            for j in range(NJ):
                S.op("act", lambda e, j=j: e.activation(out=junk.ap, in_=mtm.ap[:, j, :], func=AF.Square,
                                                         scale=RS_D, accum_out=st[:, j:j + 1]),
                     reads=mtm.ck(j), writes=junk.keys() + stat_k)
            rstd_from_ms(st[:, 0:NJ], st[:, 2:2 + NJ], st[:, 4:4 + NJ])
            for j in range(NJ):
                S.op("dve", lambda e, j=j: e.scalar_tensor_tensor(
                    out=mtm.ap[:, j, :], in0=mtm.ap[:, j, :], scalar=st[:, 4 + j:5 + j], in1=gbc.ap,
                    op0=ALU.mult, op1=ALU.mult), reads=mtm.ck(j) + stat_k + gbc.keys(), writes=mtm.ck(j))
                if final:
                    S.op("dve", lambda e, j=j: e.tensor_tensor(out=mtm.ap[:, j, :], in0=xres.ap[:, j, :],
                                                                 in1=mtm.ap[:, j, :], op=ALU.add),
                         reads=mtm.ck(j) + xres.ck(j), writes=mtm.ck(j))
                else:
                    S.op("dve", lambda e, j=j: e.tensor_tensor(out=xres.ap[:, j, :], in0=xres.ap[:, j, :],
                                                                 in1=mtm.ap[:, j, :], op=ALU.add),
                         reads=mtm.ck(j) + xres.ck(j), writes=xres.ck(j))

        tp = 0
        for kd in kinds:
            if kd != "main":
                tile_seq.append((kd, NT - n_prefix + tp))
                tp += 1
        for t in range(n_main):
            tile_seq.append(("main", t))
        tp = 0
        for kd in kinds:
            if kd == "main":
                break
            tile(kd, NT - n_prefix + tp)
            tp += 1
        conv_some(len(conv_pending))
        for t in range(n_main):
            tile("main", t)
        S.wait_all("act", [("out", t) for t in range(n_main)])
        if n_main == 0:
            S.wait_all("sp", wg.keys() + wgu.keys() + wpool.keys() + invc.keys() + colv.keys())
        assert pstate["cons"] == len(plist)

        with nc.Block() as block:
            @block.tensor
            def _(e):
                S.replay("pe", e)

            @block.scalar
            def _(e):
                S.replay("act", e)

            @block.vector
            def _(e):
                S.replay("dve", e)

            @block.gpsimd
            def _(e):
                S.replay("pool", e)

            @block.sync
            def _(e):
                S.replay("sp", e)
        info = {n: (len(e.ops), sum(len(o[0]) for o in e.ops)) for n, e in S.engs.items()}
        _DBG['S'] = S
        _DBG['bufs'] = {k_: v_ for k_, v_ in locals().items() if isinstance(v_, Buf)}
    return nc, info


def _host_constants():
    ident = np.eye(128, dtype=np.float32).astype(ml_dtypes.bfloat16)
    r = np.arange(128)
    maskT = (r[:, None] <= r[None, :]).astype(np.float32).astype(ml_dtypes.bfloat16)
    tri = np.where(r[:, None] <= r[None, :], np.float32(-1.0 / 16.0), np.float32(0.0)).astype(np.float32)
    invc = np.zeros((2, 128, 8, 16), np.float32)
    for c in range(8):
        w = 2 << (c // 2)
        invc[1, :, c, :] = 1.0 / w
        for tpos in range(16):
            invc[0, :, c, tpos] = 1.0 / min(tpos + 1, w)
    return ident, maskT, tri, invc


def _colvecs(norm_mix_pre, pool_scale, gla_norm, b_branch_gates, norm_ffn_pre):
    cvt = np.zeros((128, NCOLV), np.float32)
    cvt[:, C_GPRE:C_GPRE + 16] = norm_mix_pre.reshape(16, 128).T
    cvt[:, C_PSC:C_PSC + 8] = pool_scale.reshape(8, 128).T
    cvt[:, C_GN:C_GN + 4] = gla_norm.reshape(4, 128).T
    cvt[:, C_BG:C_BG + 32] = b_branch_gates.reshape(32, 128).T
    cvt[:, C_GFFN:C_GFFN + 16] = norm_ffn_pre.reshape(16, 128).T
    return cvt


_CACHE = {}


def prepare_inputs(x, norm_mix_pre, w_in, w_gate_up, b_gate, w_pool, pool_scale, gla_norm,
                   w_branch_a, w_branch_b, b_branch_gates, w_out, norm_mix_post,
                   norm_ffn_pre, w_ffn_gate, w_ffn_up, w_ffn_down, norm_ffn_post):
    f = lambda a: np.ascontiguousarray(np.asarray(a, dtype=np.float32))
    x = f(x)
    ident, maskT, tri, invc = _host_constants()
    invc_rest = np.ascontiguousarray(np.stack([invc[1], invc[1]]))
    wgu = np.zeros((32, 1024), np.float32)
    wgu[0:16] = f(w_gate_up)[0]
    wgu[16] = f(b_gate)[0]
    gbc = np.ascontiguousarray(np.stack([np.broadcast_to(f(norm_mix_post)[0], (128, D)),
                                         np.broadcast_to(f(norm_ffn_post)[0], (128, D))]))
    shared = {
        "w_in": f(w_in)[0], "wgu_aug": wgu, "w_pool": f(w_pool)[0],
        "w_branch_a": f(w_branch_a)[0], "w_branch_b": f(w_branch_b)[0], "w_out": f(w_out)[0],
        "w_ffn_gate": f(w_ffn_gate)[0], "w_ffn_up": f(w_ffn_up)[0], "w_ffn_down": f(w_ffn_down)[0],
        "colv": _colvecs(f(norm_mix_pre)[0], f(pool_scale)[0], f(gla_norm)[0], f(b_branch_gates)[0],
                         f(norm_ffn_pre)[0]),
        "gbc": gbc, "ident": ident, "maskT": maskT, "tri": tri,
    }
    zeros = np.zeros((TOK, D), np.float32)
    in_maps = []
    for c in range(NCORE):
        b, half = c // 2, c % 2
        m = dict(shared)
        m["x_main"] = np.ascontiguousarray(x[b, half * TOK:(half + 1) * TOK])
        m["x_prev"] = np.ascontiguousarray(x[b, 0:TOK]) if half == 1 else zeros
        m["invc"] = invc if half == 0 else invc_rest
        in_maps.append(m)
    return in_maps


def kernel(**inputs):
    if "nc" not in _CACHE:
        _CACHE["nc"] = build_program()[0]
    nc = _CACHE["nc"]
    in_maps = prepare_inputs(**inputs)
    res = run_bass_kernel_spmd(nc, in_maps, core_ids=list(range(NCORE)))
    out = np.empty((BATCH, SEQ, D), np.float32)
    for c in range(NCORE):
        b, half = c // 2, c % 2
        out[b, half * TOK:(half + 1) * TOK] = res.results[c]["out"]
    return out
```
